# Optimizing a Trainium2 kernel written in Bass

```python
import jax, jax.numpy as jnp
from jax import lax
import numpy as np

D_MODEL = 2048
BATCH = 4
SEQ = 2048
DEPTH = 4
DEC_BATCH = 8
DEC_SEQ = 1
PAST_LEN = 16384
PAGE_SIZE = 128

HEAD_DIM = 128
HEADS_PER_GROUP = D_MODEL // HEAD_DIM
DILATION_PAIRS = ((128, 1), (512, 4), (2048, 16))
N_GROUPS = len(DILATION_PAIRS)
N_SUBHEADS = N_GROUPS * HEADS_PER_GROUP
ATT_WIDTH = HEADS_PER_GROUP * HEAD_DIM
QKV_WIDTH = N_GROUPS * ATT_WIDTH
ATT_IN_COLS = 3 * QKV_WIDTH + ATT_WIDTH
BAND = 128
ATT_SCALE = HEAD_DIM ** -0.5
POOL_WINDOWS = (2, 4, 8, 16)
POOL_WIDTH = 2 * D_MODEL
POOL_GROUP = POOL_WIDTH // len(POOL_WINDOWS)
POOL_BUF = max(POOL_WINDOWS) - 1
N_BUCKETS = 32
T5_MAX_DIST = 2048
N_POOL_LAYERS = (DEPTH + 1) // 2
N_ATT_LAYERS = DEPTH // 2
RMS_EPS = 1e-6
NEG_INF = -1e30

kernel_name = "hybrid_pool_dilated_attn_decoder_step"


def rmsnorm(x, g):
    xf = x.astype(jnp.float32)
    y = xf * lax.rsqrt(jnp.mean(xf * xf, axis=-1, keepdims=True) + RMS_EPS)
    return (y * g.astype(jnp.float32)).astype(x.dtype)


def adaln(c, w, b):
    mod = jax.nn.silu(c) @ w + b
    shift, scale, gate = jnp.split(mod, 3, axis=-1)
    return shift[:, None, :], scale[:, None, :], gate[:, None, :]


def t5_bucket(dist):
    dist = np.asarray(dist, dtype=np.int64)
    max_exact = N_BUCKETS // 2
    ratio = np.log(np.maximum(dist, 1) / max_exact) / np.log(T5_MAX_DIST / max_exact)
    large = np.minimum(max_exact + (ratio * (N_BUCKETS - max_exact)).astype(np.int64), N_BUCKETS - 1)
    return np.where(dist < max_exact, dist, large).astype(np.int32)


def pool_mix(u_ext, n_prev, start_pos, w_grp, scale):
    B, T, E = u_ext.shape
    S = T - n_prev
    uf = u_ext.astype(jnp.float32)
    cs = jnp.concatenate([jnp.zeros((B, 1, E), jnp.float32), jnp.cumsum(uf, axis=1)], axis=1)
    j = np.arange(n_prev, T)
    pos = start_pos + np.arange(S)
    res = []
    for g, w in enumerate(POOL_WINDOWS):
        sl = slice(g * POOL_GROUP, (g + 1) * POOL_GROUP)
        lo = np.maximum(j + 1 - w, 0)
        cnt = np.minimum(w, pos + 1).astype(np.float32)
        csg = cs[:, :, sl]
        res.append((csg[:, j + 1] - csg[:, lo]) / cnt[None, :, None] - uf[:, n_prev:, sl])
    r = jnp.stack(res, axis=2)
    y = jnp.einsum('bsge,gef->bsgf', r, w_grp.astype(jnp.float32)).reshape(B, S, E)
    return (y * scale.astype(jnp.float32)).astype(u_ext.dtype)


def pool_layer(h, buf, w_in, w_grp, scale, w_out):
    proj = h @ w_in
    u, z = proj[..., :POOL_WIDTH], proj[..., POOL_WIDTH:]
    if buf is None:
        u_ext, n_prev, start = u, 0, 0
    else:
        u_ext = jnp.concatenate([buf.astype(u.dtype), u], axis=1)
        n_prev, start = buf.shape[1], PAST_LEN
    r = pool_mix(u_ext, n_prev, start, w_grp, scale)
    y = (r * jax.nn.silu(z)) @ w_out
    return y, u_ext[:, -POOL_BUF:]


def dilated_prompt(q, k, v, dil, tab):
    B, S, H, dh = q.shape
    n = S // dil
    nb = -(-n // BAND)
    pad = nb * BAND - n

    def sub(x):
        x = x.reshape(B, n, dil, H, dh).transpose(0, 2, 1, 3, 4)
        x = jnp.pad(x, ((0, 0), (0, 0), (0, pad), (0, 0), (0, 0)))
        return x.reshape(B, dil, nb, BAND, H, dh)

    def band(x):
        prev = jnp.pad(x, ((0, 0), (0, 0), (1, 0), (0, 0), (0, 0), (0, 0)))[:, :, :-1]
        return jnp.concatenate([prev, x], axis=3)

    qs = sub(q)
    kb, vb = band(sub(k)), band(sub(v))
    rel = np.arange(BAND)[:, None] + BAND - np.arange(2 * BAND)[None, :]
    inband = (rel >= 0) & (rel <= BAND)
    valid = inband[None] & ((np.arange(nb)[:, None, None] > 0) | (np.arange(2 * BAND)[None, None, :] >= BAND))
    bias = tab[t5_bucket(np.clip(rel, 0, BAND) * dil)].astype(jnp.float32).transpose(2, 0, 1)
    s = jnp.einsum('brcqhe,brckhe->brchqk', qs, kb).astype(jnp.float32) * ATT_SCALE + bias
    s = jnp.where(valid[:, None], s, NEG_INF)
    m = jnp.max(s, axis=-1, keepdims=True)
    p = jnp.exp(s - m)
    l = jnp.sum(p, axis=-1, keepdims=True)
    o = jnp.einsum('brchqk,brckhe->brcqhe', p / l, vb.astype(jnp.float32))
    lse = jnp.swapaxes((m + jnp.log(l))[..., 0], 3, 4)
    o = o.reshape(B, dil, nb * BAND, H, dh)[:, :, :n].transpose(0, 2, 1, 3, 4).reshape(B, S, H, dh)
    lse = lse.reshape(B, dil, nb * BAND, H)[:, :, :n].transpose(0, 2, 1, 3).reshape(B, S, H)
    return o, lse


def dilated_sample(q, k_ext, v_ext, n_buf, dil, tab):
    B, S, H, dh = q.shape
    offs = np.arange(BAND + 1)
    idx = n_buf + np.arange(S)[:, None] - offs[None, :] * dil
    valid = idx >= 0
    idx = np.maximum(idx, 0)
    kg, vg = k_ext[:, idx], v_ext[:, idx]
    bias = tab[t5_bucket(offs * dil)].astype(jnp.float32).T
    s = jnp.einsum('bshe,bskhe->bhsk', q, kg).astype(jnp.float32) * ATT_SCALE + bias[None, :, None, :]
    s = jnp.where(valid[None, None], s, NEG_INF)
    m = jnp.max(s, axis=-1, keepdims=True)
    p = jnp.exp(s - m)
    l = jnp.sum(p, axis=-1, keepdims=True)
    o = jnp.einsum('bhsk,bskhe->bshe', p / l, vg.astype(jnp.float32))
    lse = jnp.swapaxes((m + jnp.log(l))[..., 0], 1, 2)
    return o, lse


def att_layer(h, kv_bufs, w_in, w_out, t5_bias):
    B, T, _ = h.shape
    proj = h @ w_in
    shp = (B, T, N_GROUPS, HEADS_PER_GROUP, HEAD_DIM)
    q = proj[..., :QKV_WIDTH].reshape(shp)
    k = proj[..., QKV_WIDTH:2 * QKV_WIDTH].reshape(shp)
    v = proj[..., 2 * QKV_WIDTH:3 * QKV_WIDTH].reshape(shp)
    z = proj[..., 3 * QKV_WIDTH:]
    outs, lses, new_kv = [], [], []
    for g, (win, dil) in enumerate(DILATION_PAIRS):
        tab = t5_bias[:, g * HEADS_PER_GROUP:(g + 1) * HEADS_PER_GROUP]
        qg, kg, vg = q[:, :, g], k[:, :, g], v[:, :, g]
        if kv_bufs is None:
            o, lse = dilated_prompt(qg, kg, vg, dil, tab)
            keep = min(win, T)
            new_kv.append(jnp.stack([kg[:, -keep:], vg[:, -keep:]], axis=2))
        else:
            buf = kv_bufs[g]
            k_ext = jnp.concatenate([buf[:, :, 0].astype(kg.dtype), kg], axis=1)
            v_ext = jnp.concatenate([buf[:, :, 1].astype(vg.dtype), vg], axis=1)
            o, lse = dilated_sample(qg, k_ext, v_ext, buf.shape[1], dil, tab)
            new_kv.append(jnp.stack([kg, vg], axis=2))
        outs.append(o)
        lses.append(lse)
    wgt = jax.nn.softmax(jnp.stack(lses, axis=0), axis=0)
    o = jnp.sum(wgt[..., None] * jnp.stack(outs, axis=0), axis=0)
    o = o.reshape(B, T, ATT_WIDTH).astype(h.dtype)
    y = (o * jax.nn.silu(z)) @ w_out
    return y, new_kv


def setup_inputs(seed: int = 0) -> dict:
    key = jax.random.key(seed)
    ks = jax.random.split(key, 20)
    D = D_MODEL

    def nrm(k, shape, s):
        return jax.random.normal(k, shape, jnp.float32) * s

    kv_shape = lambda w: (N_ATT_LAYERS, DEC_BATCH, min(w, PAST_LEN), 2, HEADS_PER_GROUP, HEAD_DIM)
    return {
        "x_prompt": nrm(ks[0], (BATCH, SEQ, D), 1.0),
        "x_sample": nrm(ks[1], (DEC_BATCH, DEC_SEQ, D), 1.0),
        "c_prompt": nrm(ks[2], (BATCH, D), 1.0),
        "c_sample": nrm(ks[3], (DEC_BATCH, D), 1.0),
        "cache_kv0": nrm(ks[4], kv_shape(DILATION_PAIRS[0][0]), 1.0),
        "cache_kv1": nrm(ks[5], kv_shape(DILATION_PAIRS[1][0]), 1.0),
        "cache_kv2": nrm(ks[6], kv_shape(DILATION_PAIRS[2][0]), 1.0),
        "state_pool": nrm(ks[7], (N_POOL_LAYERS, DEC_BATCH, POOL_BUF, POOL_WIDTH), 1.0),
        "norm_pre": 1.0 + nrm(ks[8], (DEPTH, D), 0.05),
        "norm_post": 1.0 + nrm(ks[9], (DEPTH, D), 0.05),
        "ada_w": nrm(ks[10], (DEPTH, D, 3 * D), 0.5 * D ** -0.5),
        "ada_b": nrm(ks[11], (DEPTH, 3 * D), 0.01),
        "t5_bias": nrm(ks[12], (N_BUCKETS, N_SUBHEADS), 0.5),
        "pool_w_in": nrm(ks[13], (N_POOL_LAYERS, D, 2 * POOL_WIDTH), D ** -0.5),
        "pool_w_grp": nrm(ks[14], (N_POOL_LAYERS, len(POOL_WINDOWS), POOL_GROUP, POOL_GROUP), POOL_GROUP ** -0.5),
        "pool_scale": 1.0 + nrm(ks[15], (N_POOL_LAYERS, POOL_WIDTH), 0.1),
        "pool_w_out": nrm(ks[16], (N_POOL_LAYERS, POOL_WIDTH, D), POOL_WIDTH ** -0.5),
        "att_w_in": nrm(ks[17], (N_ATT_LAYERS, D, ATT_IN_COLS), D ** -0.5),
        "att_w_out": nrm(ks[18], (N_ATT_LAYERS, ATT_WIDTH, D), ATT_WIDTH ** -0.5),
    }


def reference(x_prompt, x_sample, c_prompt, c_sample, cache_kv0, cache_kv1, cache_kv2, state_pool,
              norm_pre, norm_post, ada_w, ada_b, t5_bias, pool_w_in, pool_w_grp, pool_scale,
              pool_w_out, att_w_in, att_w_out):
    caches = (cache_kv0, cache_kv1, cache_kv2)
    xp, xs = x_prompt, x_sample
    kv_p = [[] for _ in range(N_GROUPS)]
    kv_s = [[] for _ in range(N_GROUPS)]
    pool_p, pool_s = [], []
    for i in range(DEPTH):
        li = i // 2
        sh_p, sc_p, gt_p = adaln(c_prompt, ada_w[i], ada_b[i])
        sh_s, sc_s, gt_s = adaln(c_sample, ada_w[i], ada_b[i])
        hp = rmsnorm(xp, norm_pre[i]) * (1.0 + sc_p) + sh_p
        hs = rmsnorm(xs, norm_pre[i]) * (1.0 + sc_s) + sh_s
        if i % 2 == 0:
            yp, st_p = pool_layer(hp, None, pool_w_in[li], pool_w_grp[li], pool_scale[li], pool_w_out[li])
            ys, st_s = pool_layer(hs, state_pool[li], pool_w_in[li], pool_w_grp[li], pool_scale[li], pool_w_out[li])
            pool_p.append(st_p)
            pool_s.append(st_s)
        else:
            yp, st_p = att_layer(hp, None, att_w_in[li], att_w_out[li], t5_bias)
            ys, st_s = att_layer(hs, [c[li] for c in caches], att_w_in[li], att_w_out[li], t5_bias)
            for g in range(N_GROUPS):
                kv_p[g].append(st_p[g])
                kv_s[g].append(st_s[g])
        xp = xp + gt_p * rmsnorm(yp, norm_post[i])
        xs = xs + gt_s * rmsnorm(ys, norm_post[i])
    return (xp, xs,
            jnp.stack(kv_p[0]), jnp.stack(kv_p[1]), jnp.stack(kv_p[2]), jnp.stack(pool_p),
            jnp.stack(kv_s[0]), jnp.stack(kv_s[1]), jnp.stack(kv_s[2]), jnp.stack(pool_s))
```

```python
import contextlib
import numpy as np
import concourse.bass as bass
import concourse.mybir as mybir
from concourse.bass_utils import run_bass_kernel_spmd

F32 = mybir.dt.float32
BF16 = mybir.dt.bfloat16
AF = mybir.ActivationFunctionType
ALU = mybir.AluOpType
AX = mybir.AxisListType

T = 2048
PB = 15
NEG = -30000.0
DIL = (1, 4, 16)
WIN = (128, 512, 2048)
PWINS = (2, 4, 8, 16)
N_BUCKETS = 32
T5_MAX_DIST = 2048
RMS_EPS = 1e-6
WSLOT = 4096
FP = 384
FREP = 130
ENGS = ("pe", "act", "dve", "pool", "sp")


def t5_bucket(dist):
    dist = np.asarray(dist, dtype=np.int64)
    max_exact = N_BUCKETS // 2
    ratio = np.log(np.maximum(dist, 1) / max_exact) / np.log(T5_MAX_DIST / max_exact)
    large = np.minimum(max_exact + (ratio * (N_BUCKETS - max_exact)).astype(np.int64), N_BUCKETS - 1)
    return np.where(dist < max_exact, dist, large).astype(np.int32)


class Cfg:
    def __init__(self, D=2048, NS=2, TP=1, DEPTH=4):
        self.D, self.NS, self.TP, self.DEPTH = D, NS, TP, DEPTH
        self.KC = D // 128
        self.H = D // 128
        self.HO = self.H // TP
        self.TT = T + NS
        self.PW = 2 * D
        self.PG = self.PW // 4
        self.GC = self.PG // 128
        self.GO = 4 // TP
        self.PWo = self.PW // TP
        self.KCp = self.PWo // 128
        self.LP = (DEPTH + 1) // 2
        self.LA = DEPTH // 2


class Buf:
    __slots__ = ("name", "w", "r", "sem", "dcnt")

    def __init__(self, name):
        self.name, self.w, self.r, self.sem, self.dcnt = name, None, [], None, 0


class _Rec:
    def __init__(self):
        self.spec = None

    def __getattr__(self, name):
        def f(*a, **k):
            self.spec = (name, a, k)
            return self
        return f


def _record(fn):
    r = _Rec()
    fn(r)
    assert r.spec is not None
    return r.spec


class Prog:
    def __init__(self, nc, es):
        self.nc, self.es = nc, es
        self.streams = {e: [] for e in ENGS}
        self.esem = {e: es.enter_context(nc.semaphore("S_" + e)) for e in ENGS}
        self.cnt = {e: 0 for e in ENGS}
        self.seen = {e: {} for e in ENGS}
        self.dry = False
        self.semtab = {}
        self.nops = 0
        self.klog = False
        import os
        self.maxops = int(os.environ.get('KMAXOPS', '100000000'))

    def _bsem(self, b):
        ent = self.semtab.get(b.name)
        if ent is None:
            ent = [self.es.enter_context(self.nc.semaphore("D_" + b.name)), 0]
            self.semtab[b.name] = ent
        return ent

    def _need(self, eng, sem, val):
        if self.klog and self.seen[eng].get(id(sem), 0) < val:
            print("   WAIT", eng, [k for k, v in self.semtab.items() if v[0] is sem] or [e for e in ENGS if self.esem[e] is sem], val)
        if self.seen[eng].get(id(sem), 0) < val:
            self.seen[eng][id(sem)] = val
            self.streams[eng].append(("wait", sem, val))

    def _waits(self, eng, reads, writes, skip):
        for b in reads:
            if b.w is not None and b.w[0] is not skip:
                self._need(eng, *b.w)
        for b in writes:
            if b.w is not None and b.w[0] is not skip:
                self._need(eng, *b.w)
            for ev in b.r:
                if ev[0] is not skip:
                    self._need(eng, *ev)

    def op(self, eng, fn, reads=(), writes=(), sig=True):
        if self.dry:
            return
        self.nops += 1
        if self.nops > self.maxops:
            return
        fn = _record(fn)
        if self.klog:
            print("OP", self.nops, eng, fn[0], [getattr(a, "shape", None) for a in fn[1]], {k: (getattr(v, "shape", v)) for k, v in fn[2].items() if k in ("out", "in_", "lhsT", "rhs")})
        skip = self.esem[eng] if eng == "pe" else None
        self._waits(eng, reads, writes, skip)
        if sig:
            self.cnt[eng] += 1
            ev = (self.esem[eng], self.cnt[eng])
            self.streams[eng].append(("op", fn, self.esem[eng], 1))
        else:
            ev = (self.esem[eng], self.cnt[eng] + 1)
            self.streams[eng].append(("op", fn, None, 0))
        for b in writes:
            b.w, b.r = ev, []
        for b in reads:
            b.r.append(ev)

    def dma(self, q, fn, reads=(), writes=(), owner=None):
        if self.dry:
            return
        self.nops += 1
        if self.nops > self.maxops:
            return
        fn = _record(fn)
        if self.klog:
            print("OP", self.nops, q, "dma", {k: (getattr(v, "shape", v)) for k, v in fn[2].items() if k in ("out", "in_")})
        self._waits(q, reads, writes, None)
        ent = self._bsem(owner)
        ent[1] += 1
        sem = ent[0]
        ev = (sem, 16 * ent[1])
        self.streams[q].append(("op", fn, sem, 16))
        for b in writes:
            b.w, b.r = ev, []
        for b in reads:
            b.r.append(ev)

    def barrier(self):
        if self.dry:
            return
        for e in ENGS:
            for f in ("pe", "act", "dve", "pool"):
                if f != e and self.cnt[f] > 0:
                    self._need(e, self.esem[f], self.cnt[f])
            for ent in self.semtab.values():
                if ent[1] > 0:
                    self._need(e, ent[0], 16 * ent[1])

    def emit(self):
        nc, streams = self.nc, self.streams

        def replay(engobj, items):
            for it in items:
                if it[0] == "wait":
                    engobj.wait_ge(it[1], it[2])
                else:
                    ins = getattr(engobj, it[1][0])(*it[1][1], **it[1][2])
                    if it[2] is not None:
                        ins.then_inc(it[2], it[3])

        with nc.allow_non_contiguous_dma(reason="single sample-token columns / strided rows"), nc.Block() as block:
            @block.tensor
            def _(e):
                replay(e, streams["pe"])

            @block.scalar
            def _(e):
                replay(e, streams["act"])

            @block.vector
            def _(e):
                replay(e, streams["dve"])

            @block.gpsimd
            def _(e):
                replay(e, streams["pool"])

            @block.sync
            def _(e):
                replay(e, streams["sp"])


class Builder:
    def __init__(self, cfg):
        self.c = cfg
        self.specs = None

    def declare(self, nc, nblk):
        c = self.c
        di = lambda n, s, d=F32: nc.dram_tensor(n, list(s), d, kind="ExternalInput").ap()
        do = lambda n, s, d=F32: nc.dram_tensor(n, list(s), d, kind="ExternalOutput").ap()
        dn = lambda n, s, d=F32: nc.dram_tensor(n, list(s), d).ap()
        I = {}
        I["xp"] = di("xp", (T, c.D))
        I["xs"] = di("xs", (c.NS, c.D))
        I["cT"] = di("cT", (128, c.KC, 1 + c.NS))
        I["adab"] = di("adab", (c.DEPTH, 128, 3 * c.KC))
        I["npre"] = di("npre", (c.DEPTH, 128, c.KC))
        I["npost"] = di("npost", (c.DEPTH, 128, c.KC))
        I["t5o"] = di("t5o", (32, 3 * c.HO))
        I["oh"] = di("oh", (3, 33, FP))
        I["ohs"] = di("ohs", (3, 32, 128))
        I["fmask"] = di("fmask", (128, 3, FP))
        I["invc"] = di("invc", (128, 16))
        I["pscale"] = di("pscale", (c.LP, 128, c.KCp))
        I["spool"] = di("spool", (c.LP, c.NS, PB, c.PWo))
        for g in range(3):
            I["ck%d" % g] = di("ck%d" % g, (c.LA, c.NS, WIN[g], 2, c.HO, 128))
        I["wstream"] = di("wstream", (nblk, 128, WSLOT))
        O = {}
        O["yp"] = do("yp", (T, c.D))
        O["ys"] = do("ys", (c.NS, c.D))
        for g in range(3):
            O["kvp%d" % g] = do("kvp%d" % g, (c.LA, WIN[g], 2, c.HO, 128))
            O["kvs%d" % g] = do("kvs%d" % g, (c.LA, c.NS, 2, c.HO, 128))
        O["poolp"] = do("poolp", (c.LP, PB, c.PWo))
        O["pools"] = do("pools", (c.LP, c.NS, PB, c.PWo))
        S = {}
        S["xT"] = dn("xT", (c.KC, 128, c.TT))
        S["yT"] = dn("yT", (c.KC, 128, c.TT))
        S["gT"] = dn("gT", (max(c.KCp, c.HO), 128, c.TT), BF16)
        S["fd"] = dn("fd", (3 * c.HO, FREP * FP))
        self.I, self.O, self.S = I, O, S

    def build(self, nc, es, dry):
        c = self.c
        P = Prog(nc, es)
        P.dry = dry
        self.P, self.nc = P, nc
        if dry:
            self.specs = []
        self.wi = 0
        self.wseq = 0
        I, O, S = self.I, self.O, self.S
        KC, NS, TT, HO = c.KC, c.NS, c.TT, c.HO
        sbt = lambda n, s, d: es.enter_context(nc.sbuf_tensor("s_" + n, list(s), d))
        pst = lambda n, s, d: es.enter_context(nc.psum_tensor("p_" + n, list(s), d))
        B = lambda n: Buf(n)

        hraw = sbt("hraw", (128, KC * TT + 128), BF16); hTB = B("hT")
        hT = hraw[:, 0:KC * TT].rearrange("p (k t) -> p k t", k=KC)
        stage = [sbt("wst%d" % i, (128, WSLOT), F32) for i in range(2)]
        stB = [B("wst%d" % i) for i in range(2)]
        NWB = 3
        wbf = [sbt("wbf%d" % i, (128, WSLOT), BF16) for i in range(NWB)]
        wbB = [B("wbf%d" % i) for i in range(NWB)]
        idf = sbt("idf", (128, 128), F32); idb = sbt("idb", (128, 128), BF16)
        onesb = sbt("onesb", (128, 128), BF16); onesD = sbt("onesD", (128, 128), F32)
        onesf = sbt("onesf", (128, 128), F32)
        epsb = sbt("epsb", (128, 1), F32)
        cB = B("consts")
        cTs = sbt("cTs", (128, KC, 1 + NS), F32); scT = sbt("scT", (128, KC, 1 + NS), BF16); scB = B("scT")
        adab = sbt("adabs", (128, c.DEPTH, 3 * KC), F32)
        npre = sbt("npres", (128, c.DEPTH, KC), F32)
        npost = sbt("nposts", (128, c.DEPTH, KC), F32)
        vecB = B("vecs")
        modT = [sbt("modT%d" % i, (128, 3 * KC, 1 + NS), F32) for i in range(c.DEPTH)]
        modB = [B("modT%d" % i) for i in range(c.DEPTH)]
        tabA = [sbt("tabA%d" % i, (128, KC, 1 + NS), F32) for i in range(c.DEPTH)]
        tabG = [sbt("tabG%d" % i, (128, KC, 1 + NS), F32) for i in range(c.DEPTH)]
        tabB = [B("tab%d" % i) for i in range(c.DEPTH)]
        t5s = sbt("t5s", (32, 3 * HO), F32)
        biasS = sbt("biasS", (128, 3, HO), F32)
        invc = sbt("invc", (128, 16), F32)
        pscale = sbt("pscales", (128, c.LP, c.KCp), F32)
        ARENA = 19200
        arena = sbt("arena", (128, ARENA), F32)
        self.aoff = 0

        def areset():
            self.aoff = 0

        def aview(shape, dt):
            n = int(np.prod(shape))
            nel = n if dt == F32 else (n + 1) // 2
            off = self.aoff
            self.aoff += nel
            assert self.aoff <= ARENA, ("arena overflow", self.aoff)
            v = arena[:, off:off + nel]
            if dt == BF16:
                v = v.bitcast(BF16)[:, 0:n]
            if len(shape) == 2:
                v = v.rearrange("p (a b) -> p a b", a=shape[0])
            elif len(shape) == 3:
                v = v.rearrange("p (a b c) -> p a b c", a=shape[0], b=shape[1])
            return v

        pj = [pst("pj%d" % i, (128, 512), F32) for i in range(3)]; pjB = [B("pj%d" % i) for i in range(3)]
        sps = [pst("sps%d" % i, (128, 512), F32) for i in range(2)]; spsB = [B("sps%d" % i) for i in range(2)]
        ulp = [pst("ulp%d" % i, (128, 512), F32) for i in range(2)]; ulpB = [B("ulp%d" % i) for i in range(2)]
        tpp = pst("tpp", (128, 512), F32); tppB = B("tpp")
        rot = {"pj": 0, "sps": 0, "ulp": 0}

        def nxt(kind):
            lst, bl = {"pj": (pj, pjB), "sps": (sps, spsB), "ulp": (ulp, ulpB)}[kind]
            i = rot[kind] % len(lst)
            rot[kind] += 1
            return lst[i], bl[i]

        ptiles = [(t0, 512) for t0 in range(0, T, 512)]
        alltiles = ptiles + [(T, NS)]

        def wload(n):
            sp = self.specs[n]
            ne = sp["kcb"] * sp["C"]
            s = n % 2
            P.dma("sp", lambda e, s=s, n=n, ne=ne: e.dma_start(out=stage[s][:, 0:ne], in_=I["wstream"][n, :, 0:ne]),
                  writes=[stB[s]], owner=stB[s])

        def wcast(n):
            sp = self.specs[n]
            ne = sp["kcb"] * sp["C"]
            s, d = n % 2, n % NWB
            P.op("pool", lambda e, s=s, d=d, ne=ne: e.tensor_copy(out=wbf[d][:, 0:ne], in_=stage[s][:, 0:ne]),
                 reads=[stB[s]], writes=[wbB[d]])

        def wget(src, layer, r0, nrows, cols):
            kcb, C = nrows // 128, len(cols)
            assert kcb * C <= WSLOT
            if dry:
                self.specs.append(dict(src=src, layer=layer, r0=r0, nrows=nrows, cols=np.asarray(cols), kcb=kcb, C=C))
                n = len(self.specs) - 1
            else:
                n = self.wi
                self.wi += 1
                nb_ = len(self.specs)
                seq_end = 2 * n + 4
                while self.wseq <= seq_end:
                    q = self.wseq
                    self.wseq += 1
                    if q < 2:
                        if q < nb_:
                            wload(q)
                    elif q % 2 == 0:
                        k = (q - 2) // 2
                        if k < nb_:
                            wcast(k)
                    else:
                        k = (q - 3) // 2 + 2
                        if k < nb_:
                            wload(k)
            d = n % NWB
            return wbf[d][:, 0:kcb * C].rearrange("p (k c) -> p k c", k=kcb), wbB[d]

        def setup():
            P.op("pool", lambda e: e.memset(idf[:], 0.0), writes=[cB])
            P.op("pool", lambda e: e.affine_select(out=idf[:], in_=idf[:], pattern=[[-1, 128]], compare_op=ALU.not_equal,
                                                   fill=1.0, base=0, channel_multiplier=1), reads=[cB], writes=[cB])
            P.op("pool", lambda e: e.tensor_copy(out=idb[:], in_=idf[:]), reads=[cB], writes=[cB])
            P.op("pool", lambda e: e.memset(onesb[:], 1.0), writes=[cB])
            P.op("pool", lambda e: e.memset(onesf[:], 1.0), writes=[cB])
            P.op("pool", lambda e: e.memset(onesD[:], 1.0 / c.D), writes=[cB])
            P.op("pool", lambda e: e.memset(epsb[:], RMS_EPS), writes=[cB])
            P.dma("sp", lambda e: e.dma_start(out=cTs[:], in_=I["cT"]), writes=[scB], owner=scB)
            P.op("act", lambda e: e.activation(out=scT[:], in_=cTs[:], func=AF.Silu), reads=[scB], writes=[scB])
            P.dma("sp", lambda e: e.dma_start(out=adab[:], in_=I["adab"].rearrange("l p k -> p l k")), writes=[vecB], owner=vecB)
            P.dma("sp", lambda e: e.dma_start(out=npre[:], in_=I["npre"].rearrange("l p k -> p l k")), writes=[vecB], owner=vecB)
            P.dma("sp", lambda e: e.dma_start(out=npost[:], in_=I["npost"].rearrange("l p k -> p l k")), writes=[vecB], owner=vecB)
            P.dma("sp", lambda e: e.dma_start(out=pscale[:], in_=I["pscale"].rearrange("l p k -> p l k")), writes=[vecB], owner=vecB)
            P.dma("sp", lambda e: e.dma_start(out=invc[:], in_=I["invc"]), writes=[vecB], owner=vecB)
            P.dma("sp", lambda e: e.dma_start(out=t5s[:], in_=I["t5o"]), writes=[vecB], owner=vecB)
            areset()
            ohsb = aview((3, FP), F32)
            ohss = aview((3, 128), F32)
            fsb = aview((3, FP), F32)[0:HO]
            fmk = aview((3, FP), F32)
            tB = B("t5tmp")
            P.dma("sp", lambda e: e.dma_start(out=ohsb[0:33], in_=I["oh"].rearrange("g b j -> b g j")), writes=[tB], owner=tB)
            P.dma("sp", lambda e: e.dma_start(out=ohss[0:32], in_=I["ohs"].rearrange("g b j -> b g j")), writes=[tB], owner=tB)
            P.dma("sp", lambda e: e.dma_start(out=fmk, in_=I["fmask"]), writes=[tB], owner=tB)
            fB = B("fsb")
            for g in range(3):
                ps, psB = nxt("pj")
                P.op("pe", lambda e, ps=ps, g=g: e.matmul(ps[0:HO, 0:FP], lhsT=t5s[:, g * HO:(g + 1) * HO], rhs=ohsb[0:32, g, :],
                                                           start=True, stop=True), reads=[vecB, tB], writes=[psB])
                P.op("dve", lambda e, ps=ps, g=g: e.tensor_tensor(out=fsb[:, g, :], in0=ps[0:HO, 0:FP], in1=fmk[0:HO, g, :], op=ALU.add),
                     reads=[psB, tB], writes=[fB])
                ps2, ps2B = nxt("pj")
                P.op("pe", lambda e, ps2=ps2, g=g: e.matmul(ps2[:, 0:HO], lhsT=ohss[0:32, g, :], rhs=t5s[:, g * HO:(g + 1) * HO],
                                                             start=True, stop=True), reads=[vecB, tB], writes=[ps2B])
                P.op("dve", lambda e, ps2=ps2, g=g: e.tensor_copy(out=biasS[:, g, :], in_=ps2[:, 0:HO]), reads=[ps2B], writes=[vecB])
            self.fdB = B("fd")
            import os
            for g in range(3):
                if os.environ.get("KSKIP_FD"):
                    break
                src = bass.AP(fsb.tensor, fsb[:, g, :].offset, [list(fsb.ap[0]), [0, FREP], [1, FP]])
                dst = S["fd"][g * HO:(g + 1) * HO, :].rearrange("h (r j) -> h r j", r=FREP)
                P.dma("sp", lambda e, src=src, dst=dst: e.dma_start(out=dst, in_=src), reads=[fB], writes=[self.fdB], owner=fB)

        def adaln(i):
            nb = 3 * c.D // 256
            for b in range(nb):
                wv, wB = wget("ada_w", i, 0, c.D, np.arange(b * 256, (b + 1) * 256))
                for m in range(2):
                    mc = b * 2 + m
                    ps, psB = nxt("pj")
                    for kc in range(KC):
                        P.op("pe", lambda e, ps=ps, wv=wv, kc=kc, m=m: e.matmul(
                            ps[:, 0:1 + NS], lhsT=wv[:, kc, m * 128:(m + 1) * 128], rhs=scT[:, kc, :],
                            start=(kc == 0), stop=(kc == KC - 1)), reads=[wB, scB], writes=[psB], sig=(kc == KC - 1))
                    P.op("dve", lambda e, ps=ps, mc=mc: e.tensor_scalar(
                        out=modT[i][:, mc, :], in0=ps[:, 0:1 + NS], scalar1=adab[:, i, mc:mc + 1], scalar2=0.0, op0=ALU.add, op1=ALU.add),
                        reads=[psB, vecB], writes=[modB[i]])
            a_ = npre[:, i, :]; b_ = npost[:, i, :]
            npb = bass.AP(a_.tensor, a_.offset, [list(a_.ap[0]), [1, KC], [0, 1 + NS]])
            npo = bass.AP(b_.tensor, b_.offset, [list(b_.ap[0]), [1, KC], [0, 1 + NS]])
            P.op("dve", lambda e: e.scalar_tensor_tensor(out=tabA[i][:], in0=modT[i][:, KC:2 * KC, :], scalar=1.0, in1=npb,
                                                         op0=ALU.add, op1=ALU.mult), reads=[modB[i], vecB], writes=[tabB[i]])
            P.op("dve", lambda e: e.tensor_tensor(out=tabG[i][:], in0=modT[i][:, 2 * KC:3 * KC, :], in1=npo, op=ALU.mult),
                 reads=[modB[i], vecB], writes=[tabB[i]])

        xTB = {}
        yTB = {}

        def dB(dct, key, nm):
            if key not in dct:
                dct[key] = B("%s_%s" % (nm, key))
            return dct[key]

        def rms_stats(src, n, sq, red, rstd, srcB, tmpB):
            P.op("act", lambda e: e.activation(out=sq[:, :, 0:n], in_=src[:, :, 0:n], func=AF.Square), reads=[srcB], writes=[tmpB])
            P.op("dve", lambda e: e.tensor_reduce(out=red[:, 0:n], in_=sq[:, :, 0:n].rearrange("p k t -> p t k"),
                                                  axis=AX.X, op=ALU.add), reads=[tmpB], writes=[tmpB])
            ps, psB = nxt("pj")
            P.op("pe", lambda e: e.matmul(ps[:, 0:n], lhsT=onesD[:], rhs=red[:, 0:n], start=True, stop=True),
                 reads=[tmpB, cB], writes=[psB])
            P.op("act", lambda e: e.activation(out=rstd[:, 0:n], in_=ps[:, 0:n], func=AF.Sqrt, bias=epsb[:, 0:1], scale=1.0),
                 reads=[psB, cB], writes=[tmpB])
            P.op("dve", lambda e: e.reciprocal(out=rstd[:, 0:n], in_=rstd[:, 0:n]), reads=[tmpB], writes=[tmpB])

        def bc_k(v, n):
            return bass.AP(v.tensor, v.offset, [list(v.ap[0]), [0, KC], [1, n]])

        def norm_phase(i):
            P.barrier()
            areset()
            NT = 256
            xt = [aview((KC, NT), F32) for _ in range(2)]; xtB = [B("xt0"), B("xt1")]
            yt = aview((KC, NT), F32); ytB = B("yt")
            sq = aview((KC, NT), F32)
            red = aview((1, NT), F32)[:, 0, :]
            rstd = aview((1, NT), F32)[:, 0, :]
            tmpB = B("nrm_tmp")
            xin = aview((1, c.D), F32)[:, 0, :] if i in (0, c.DEPTH) else None
            xinB = B("xin")
            tiles = [(t0, NT, 0) for t0 in range(0, T, NT)] + [(T + s, 1, 1 + s) for s in range(NS)]
            for ti, (t0, n, r) in enumerate(tiles):
                x_, xB_ = xt[ti % 2], xtB[ti % 2]
                if i == 0:
                    nsub = (n + 127) // 128
                    for sub in range(nsub):
                        m = min(128, n - sub * 128)
                        srcrows = I["xp"][t0 + sub * 128:t0 + sub * 128 + m, :] if t0 < T else I["xs"][t0 - T:t0 - T + 1, :]
                        P.dma("sp", lambda e, srcrows=srcrows, m=m: e.dma_start(out=xin[0:m, :], in_=srcrows), writes=[xinB], owner=xinB)
                        for k4 in range(0, KC, 4):
                            nk = min(4, KC - k4)
                            for kk in range(nk):
                                P.op("pe", lambda e, kk=kk, k4=k4, m=m: e.transpose(tpp[:, kk * 128:kk * 128 + m], xin[0:m, (k4 + kk) * 128:(k4 + kk + 1) * 128], idf[0:m, 0:m]),
                                     reads=[xinB, cB], writes=[tppB], sig=(kk == nk - 1))
                            P.op("dve", lambda e, x_=x_, k4=k4, nk=nk, m=m, sub=sub: e.tensor_copy(
                                out=x_[:, k4:k4 + nk, sub * 128:sub * 128 + m],
                                in_=tpp[:, 0:nk * 128].rearrange("p (k t) -> p k t", k=nk)[:, :, 0:m]), reads=[tppB], writes=[xB_])
                else:
                    yb = dB(yTB, (t0 // 512) if t0 < T else 4 + (t0 - T), "yT"); xb = dB(xTB, ti, "xT")
                    P.dma("sp", lambda e, t0=t0, n=n: e.dma_start(out=yt[:, :, 0:n], in_=S["yT"][:, :, t0:t0 + n].rearrange("k p t -> p k t")),
                          reads=[yb], writes=[ytB], owner=ytB)
                    P.dma("sp", lambda e, t0=t0, n=n, x_=x_: e.dma_start(out=x_[:, :, 0:n], in_=S["xT"][:, :, t0:t0 + n].rearrange("k p t -> p k t")),
                          reads=[xb], writes=[xB_], owner=xB_)
                    rms_stats(yt, n, sq, red, rstd, ytB, tmpB)
                    P.op("dve", lambda e, n=n: e.tensor_tensor(out=yt[:, :, 0:n], in0=yt[:, :, 0:n], in1=bc_k(rstd[:, 0:n], n), op=ALU.mult),
                         reads=[ytB, tmpB], writes=[ytB])
                    for kc in range(KC):
                        P.op("dve", lambda e, kc=kc, n=n, x_=x_, r=r: e.scalar_tensor_tensor(
                            out=x_[:, kc, 0:n], in0=yt[:, kc, 0:n], scalar=tabG[i - 1][:, kc, r:r + 1], in1=x_[:, kc, 0:n],
                            op0=ALU.mult, op1=ALU.add), reads=[ytB, xB_, tabB[i - 1]], writes=[xB_])
                if i < c.DEPTH:
                    xb = dB(xTB, ti, "xT")
                    P.dma("sp", lambda e, t0=t0, n=n, x_=x_: e.dma_start(out=S["xT"][:, :, t0:t0 + n].rearrange("k p t -> p k t"), in_=x_[:, :, 0:n]),
                          reads=[xB_], writes=[xb], owner=xB_)
                    rms_stats(x_, n, sq, red, rstd, xB_, tmpB)
                    P.op("dve", lambda e, n=n, x_=x_: e.tensor_tensor(out=yt[:, :, 0:n], in0=x_[:, :, 0:n], in1=bc_k(rstd[:, 0:n], n), op=ALU.mult),
                         reads=[xB_, tmpB], writes=[ytB])
                    for kc in range(KC):
                        P.op("act", lambda e, kc=kc, n=n, t0=t0, r=r: e.activation(
                            out=hT[:, kc, t0:t0 + n], in_=yt[:, kc, 0:n], func=AF.Identity,
                            scale=tabA[i][:, kc, r:r + 1], bias=modT[i][:, kc, r:r + 1]), reads=[ytB, tabB[i], modB[i]], writes=[hTB])
                else:
                    nsub = (n + 127) // 128
                    for sub in range(nsub):
                        m = min(128, n - sub * 128)
                        for k4 in range(0, KC, 4):
                            nk = min(4, KC - k4)
                            for kk in range(nk):
                                P.op("pe", lambda e, kk=kk, k4=k4, m=m, sub=sub, x_=x_: e.transpose(
                                    tpp[0:m, kk * 128:(kk + 1) * 128], x_[:, k4 + kk, sub * 128:sub * 128 + m], idf[:]),
                                    reads=[xB_, cB], writes=[tppB], sig=(kk == nk - 1))
                            P.op("dve", lambda e, k4=k4, nk=nk, m=m: e.tensor_copy(out=xin[0:m, k4 * 128:(k4 + nk) * 128], in_=tpp[0:m, 0:nk * 128]),
                                 reads=[tppB], writes=[xinB])
                        dst = O["yp"][t0 + sub * 128:t0 + sub * 128 + m, :] if t0 < T else O["ys"][t0 - T:t0 - T + 1, :]
                        P.dma("sp", lambda e, dst=dst, m=m: e.dma_start(out=dst, in_=xin[0:m, :]), reads=[xinB], writes=[self.outB], owner=xinB)

        def feat_proj(wv, wB, coff, kcn, rhs_fn, rhsB, evac, tiles=alltiles):
            for (t0, n) in tiles:
                ps, psB = nxt("pj")
                for kc in range(kcn):
                    P.op("pe", lambda e, ps=ps, kc=kc, t0=t0, n=n: e.matmul(
                        ps[:, 0:n], lhsT=wv[:, kc, coff:coff + 128], rhs=rhs_fn(kc, t0, n), start=(kc == 0), stop=(kc == kcn - 1)),
                        reads=[wB, rhsB], writes=[psB], sig=(kc == kcn - 1))
                evac(ps, psB, t0, n)

        def tok_proj(wv, wB, c0, ncols, colsel, m, evac):
            ps, psB = nxt("pj")
            for kc in range(KC):
                P.op("pe", lambda e, ps=ps, kc=kc: e.matmul(ps[0:m, 0:ncols], lhsT=colsel(kc), rhs=wv[:, kc, c0:c0 + ncols],
                                                            start=(kc == 0), stop=(kc == KC - 1)),
                     reads=[wB, hTB], writes=[psB], sig=(kc == KC - 1))
            evac(ps, psB)

        def wout_phase(src, li, kcg, npass):
            P.barrier()
            areset()
            yst = [aview((1, 512), F32)[:, 0, :] for _ in range(3)]; ystB = [B("yst%d" % k) for k in range(3)]
            Cw = WSLOT // (kcg * 128) * 128
            Cw = min(Cw, 256)
            nbw = c.D // Cw
            cnt = 0
            for ps_i in range(npass):
                if npass == 1:
                    tl = alltiles; c0, c1 = 0, TT
                elif ps_i == 0:
                    tl = ptiles[0:2]; c0, c1 = 0, 1024
                else:
                    tl = ptiles[2:4] + [(T, NS)]; c0, c1 = 1024, TT
                w = c1 - c0
                assert kcg * w <= KC * TT + 128
                gin = hraw[:, 0:kcg * w].rearrange("p (k t) -> p k t", k=kcg)
                for k in range(kcg):
                    P.dma("sp", lambda e, k=k, c0=c0, c1=c1, gin=gin: e.dma_start(out=gin[:, k, :], in_=S["gT"][k, :, c0:c1]),
                          reads=[self.gTB[k]], writes=[hTB], owner=hTB)
                for b in range(nbw):
                    wv, wB = wget(src, li, 0, kcg * 128, np.arange(b * Cw, (b + 1) * Cw))
                    for m in range(Cw // 128):
                        mc = b * (Cw // 128) + m

                        def ev(ps, psB, t0, n, mc=mc):
                            nonlocal cnt
                            k = cnt % 3; cnt += 1
                            if cnt % 2:
                                P.op("act", lambda e: e.activation(out=yst[k][:, 0:n], in_=ps[:, 0:n], func=AF.Copy), reads=[psB], writes=[ystB[k]])
                            else:
                                P.op("dve", lambda e: e.tensor_copy(out=yst[k][:, 0:n], in_=ps[:, 0:n]), reads=[psB], writes=[ystB[k]])
                            ti = (t0 // 512) if t0 < T else None
                            dsts = [dB(yTB, ti, "yT")] if ti is not None else [dB(yTB, 4 + s, "yT") for s in range(NS)]
                            P.dma("sp", lambda e, k=k, t0=t0, n=n, mc=mc: e.dma_start(out=S["yT"][mc, :, t0:t0 + n], in_=yst[k][:, 0:n]),
                                  reads=[ystB[k]], writes=dsts, owner=ystB[k])
                        feat_proj(wv, wB, m * 128, kcg, lambda kc, t0, n, gin=gin, c0=c0: gin[:, kc, t0 - c0:t0 - c0 + n], hTB, ev, tiles=tl)

        def pool_phase(i):
            li = i // 2
            P.barrier()
            areset()
            TP16 = 16 + T
            ub = aview((1, TP16), F32)[:, 0, :]; ubB = B("ub")
            sa = aview((1, TP16), F32)[:, 0, :]; saB = B("sa")
            sb_ = aview((1, TP16), F32)[:, 0, :]; sbB = B("sb")
            rT = aview((c.GC, TT), BF16); rTB = B("rT")
            sz = [aview((1, 512), F32)[:, 0, :] for _ in range(2)]; szB = [B("sz0"), B("sz1")]
            gst = [aview((1, TT), BF16)[:, 0, :] for _ in range(2)]; gstB = [B("gst0"), B("gst1")]
            ust = [aview((1, 128), F32)[:, 0, :] for _ in range(2)]; ustB = [B("ust0"), B("ust1")]
            stt = aview((1, 128), F32)[:, 0, :]; sttB = B("stt")
            uext = aview((NS, 16), F32); uextB = B("uext")
            rs_ = aview((1, NS), F32)[:, 0, :]
            P.op("pool", lambda e: e.memset(ub[:, 0:16], 0.0), writes=[ubB])
            P.op("pool", lambda e: e.memset(sa[:, 0:16], 0.0), writes=[saB])
            P.op("pool", lambda e: e.memset(sb_[:, 0:16], 0.0), writes=[sbB])
            cpB = B("poolcp")
            for s in range(NS):
                P.dma("sp", lambda e, s=s: e.dma_start(out=O["pools"][li, s, 0:PB - 1, :], in_=I["spool"][li, s, 1:PB, :]),
                      writes=[self.outB], owner=cpB)
            self.gTB = [B("gT%d" % k) for k in range(c.KCp)]
            ucnt = 0
            self.zcnt = 0
            for go in range(c.GO):
                gw = self.pool_groups[go]
                w = PWINS[gw]
                for fc in range(c.GC):
                    fo = go * c.GC + fc
                    if fc % 2 == 0:
                        nch = min(2, c.GC - fc)
                        cols = np.concatenate([np.arange(gw * c.PG + (fc + q) * 128, gw * c.PG + (fc + q + 1) * 128) for q in range(nch)])
                        wv, wB = wget("pool_w_in", li, 0, c.D, cols)
                    coff = (fc % 2) * 128

                    def ev_u(ps, psB, t0, n):
                        if t0 < T:
                            P.op("act", lambda e, ps=ps, t0=t0, n=n: e.activation(out=ub[:, 16 + t0:16 + t0 + n], in_=ps[:, 0:n], func=AF.Copy),
                                 reads=[psB], writes=[ubB])
                        else:
                            for s in range(NS):
                                P.op("act", lambda e, ps=ps, s=s: e.activation(out=uext[:, s, 15:16], in_=ps[:, s:s + 1], func=AF.Copy),
                                     reads=[psB], writes=[uextB])
                    feat_proj(wv, wB, coff, KC, lambda kc, t0, n: hT[:, kc, t0:t0 + n], hTB, ev_u)
                    k = ucnt % 2; ucnt += 1

                    def ev_rows(ps, psB, k=k, fo=fo):
                        P.op("dve", lambda e, ps=ps, k=k: e.tensor_copy(out=ust[k][0:PB + NS, :], in_=ps[0:PB + NS, 0:128]), reads=[psB], writes=[ustB[k]])
                        P.dma("sp", lambda e, k=k, fo=fo: e.dma_start(out=O["poolp"][li, :, fo * 128:(fo + 1) * 128], in_=ust[k][0:PB, :]),
                              reads=[ustB[k]], writes=[self.outB], owner=ustB[k])
                        for s in range(NS):
                            P.dma("sp", lambda e, k=k, fo=fo, s=s: e.dma_start(out=O["pools"][li, s, PB - 1:PB, fo * 128:(fo + 1) * 128],
                                                                               in_=ust[k][PB + s:PB + s + 1, :]),
                                  reads=[ustB[k]], writes=[self.outB], owner=ustB[k])
                    tok_proj(wv, wB, coff, 128, lambda kc: hT[:, kc, T - PB:T + NS], PB + NS, ev_rows)
                    for s in range(NS):
                        P.dma("sp", lambda e, s=s, fo=fo: e.dma_start(out=stt[0:PB, :], in_=I["spool"][li, s, :, fo * 128:(fo + 1) * 128]),
                              writes=[sttB], owner=sttB)
                        P.op("pe", lambda e: e.transpose(tpp[:, 0:PB], stt[0:PB, :], idf[0:PB, 0:PB]), reads=[sttB, cB], writes=[tppB])
                        P.op("dve", lambda e, s=s: e.tensor_copy(out=uext[:, s, 0:PB], in_=tpp[:, 0:PB]), reads=[tppB], writes=[uextB])
                    cur, curB = ub, ubB
                    bufs = [(sa, saB), (sb_, sbB)]
                    sh = 1
                    bi = 0
                    while sh < w:
                        dst, dstB = bufs[bi % 2]; bi += 1
                        P.op("dve" if bi % 2 else "pool", lambda e, dst=dst, cur=cur, sh=sh: e.tensor_tensor(
                            out=dst[:, 16:TP16], in0=cur[:, 16:TP16], in1=cur[:, 16 - sh:TP16 - sh], op=ALU.add),
                            reads=[curB], writes=[dstB])
                        cur, curB = dst, dstB
                        sh *= 2
                    fin, finB = bufs[bi % 2]
                    P.op("dve", lambda e, fin=fin, cur=cur, w=w: e.scalar_tensor_tensor(
                        out=fin[:, 16:TP16], in0=cur[:, 16:TP16], scalar=1.0 / w, in1=ub[:, 16:TP16], op0=ALU.mult, op1=ALU.subtract),
                        reads=[curB, ubB], writes=[finB])
                    P.op("dve", lambda e, fin=fin, cur=cur, w=w: e.tensor_tensor(out=fin[:, 16:16 + w - 1], in0=cur[:, 16:16 + w - 1], in1=invc[:, 0:w - 1], op=ALU.mult),
                         reads=[curB, vecB], writes=[finB])
                    P.op("dve", lambda e, fin=fin, w=w: e.tensor_tensor(out=fin[:, 16:16 + w - 1], in0=fin[:, 16:16 + w - 1], in1=ub[:, 16:16 + w - 1], op=ALU.subtract),
                         reads=[ubB], writes=[finB])
                    P.op("act", lambda e, fin=fin, fc=fc: e.activation(out=rT[:, fc, 0:T], in_=fin[:, 16:TP16], func=AF.Copy), reads=[finB], writes=[rTB])
                    P.op("dve", lambda e, w=w: e.tensor_reduce(out=rs_[:, 0:NS], in_=uext[:, :, 16 - w:16], axis=AX.X, op=ALU.add), reads=[uextB], writes=[uextB])
                    P.op("dve", lambda e, w=w, fc=fc: e.scalar_tensor_tensor(out=rT[:, fc, T:TT], in0=rs_[:, 0:NS], scalar=1.0 / w, in1=uext[:, :, 15],
                                                                             op0=ALU.mult, op1=ALU.subtract), reads=[uextB], writes=[rTB])
                Cg = min(256, WSLOT // c.GC // 128 * 128, c.PG)
                for bg in range(c.PG // Cg):
                    wg, wgB = wget("pool_w_grp", (li, gw), 0, c.PG, np.arange(bg * Cg, (bg + 1) * Cg))
                    for zb in range(0, Cg // 128, 2):
                        nz = min(2, Cg // 128 - zb)
                        zc0 = c.PW + gw * c.PG + bg * Cg + zb * 128
                        wz, wzB = wget("pool_w_in", li, 0, c.D, np.arange(zc0, zc0 + nz * 128))
                        for q in range(nz):
                            m = zb + q
                            fo = go * c.GC + bg * (Cg // 128) + m
                            k = fo % 2
                            for (t0, n) in alltiles:
                                pa, paB = nxt("pj")
                                for kc in range(c.GC):
                                    P.op("pe", lambda e, pa=pa, kc=kc, m=m, t0=t0, n=n: e.matmul(
                                        pa[:, 0:n], lhsT=wg[:, kc, m * 128:(m + 1) * 128], rhs=rT[:, kc, t0:t0 + n],
                                        start=(kc == 0), stop=(kc == c.GC - 1)), reads=[wgB, rTB], writes=[paB], sig=(kc == c.GC - 1))
                                pz, pzB = nxt("pj")
                                for kc in range(KC):
                                    P.op("pe", lambda e, pz=pz, kc=kc, q=q, t0=t0, n=n: e.matmul(
                                        pz[:, 0:n], lhsT=wz[:, kc, q * 128:(q + 1) * 128], rhs=hT[:, kc, t0:t0 + n],
                                        start=(kc == 0), stop=(kc == KC - 1)), reads=[wzB, hTB], writes=[pzB], sig=(kc == KC - 1))
                                zi = self.zcnt % 2; self.zcnt += 1
                                P.op("act", lambda e, pz=pz, zi=zi, n=n: e.activation(out=sz[zi][:, 0:n], in_=pz[:, 0:n], func=AF.Silu),
                                     reads=[pzB], writes=[szB[zi]])
                                P.op("dve", lambda e, pa=pa, zi=zi, n=n, t0=t0, k=k, fo=fo: e.scalar_tensor_tensor(
                                    out=gst[k][:, t0:t0 + n], in0=pa[:, 0:n], scalar=pscale[:, li, fo:fo + 1], in1=sz[zi][:, 0:n],
                                    op0=ALU.mult, op1=ALU.mult), reads=[paB, szB[zi], vecB], writes=[gstB[k]])
                            P.dma("sp", lambda e, k=k, fo=fo: e.dma_start(out=S["gT"][fo, :, :], in_=gst[k][:, :]),
                                  reads=[gstB[k]], writes=[self.gTB[fo]], owner=gstB[k])

        def att_phase(i):
            la = i // 2
            P.barrier()
            areset()
            QT = aview((3, TT), BF16); QTB = B("QT")
            siluz = aview((1, TT), BF16)[:, 0, :]; szB_ = B("siluz")
            acc = aview((2, TT), F32); accB = B("acc")
            KV = aview((16, 256), BF16); KVB = B("KV")
            KTt = aview((16, 128), BF16); KTB = B("KT")
            KVs = aview((NS, 256), BF16); KVsB = B("KVs")
            gst = aview((1, TT), BF16)[:, 0, :]; gstB = B("gst")
            biasT = aview((3, 256), F32); biasB = B("biasT")
            Ssb = [aview((1, 256), F32)[:, 0, :] for _ in range(2)]; SsbB = [B("Ssb0"), B("Ssb1")]
            PT = [aview((1, 256), BF16)[:, 0, :] for _ in range(2)]; PTB = [B("PT0"), B("PT1")]
            kvst = [aview((1, 256), F32)[:, 0, :] for _ in range(3)]; kvstB = [B("kvst%d" % k) for k in range(3)]
            cch = [aview((1, 256), F32)[:, 0, :] for _ in range(2)]; cchB = [B("cch0"), B("cch1")]
            cchb = [aview((1, 256), BF16)[:, 0, :] for _ in range(2)]; cchbB = [B("cchb0"), B("cchb1")]
            kcT = aview((1, 128), BF16)[:, 0, :]; kcTB = B("kcT")
            knT = aview((1, 4), BF16)[:, 0, :]; knTB = B("knT")
            pts = aview((1, 4), BF16)[:, 0, :]; ptsB = B("pts")
            rl = aview((1, TT), F32)[:, 0, :]; rlB = B("rl")
            self.gTB = [B("gT%d" % k) for k in range(HO)]
            blk_i = 0
            kv_i = 0
            c_i = 0
            for ho in range(HO):
                hg = self.heads[ho]
                for g in range(3):
                    row = g * HO + ho
                    src = bass.AP(S["fd"].tensor, S["fd"][row, :].offset + 127, [[FP - 1, 128], [128, 2], [1, 128]])
                    P.dma("sp", lambda e, g=g, src=src: e.dma_start(out=biasT[:, g, :].rearrange("p (a q) -> p a q", a=2), in_=src),
                          reads=[self.fdB], writes=[biasB], owner=biasB)
                for (blk, items) in ((0, (("q", 0), ("q", 1))), (1, (("q", 2), ("z", 0)))):
                    wv, wB = wget("att_w_in", la, 0, c.D, self.att_cols(hg, blk))
                    for q, (kind, g) in enumerate(items):
                        if kind == "q":
                            def ev(ps, psB, t0, n, g=g):
                                P.op("act", lambda e, ps=ps, t0=t0, n=n, g=g: e.activation(out=QT[:, g, t0:t0 + n], in_=ps[:, 0:n], func=AF.Copy, scale=128 ** -0.5),
                                     reads=[psB], writes=[QTB])
                        else:
                            def ev(ps, psB, t0, n):
                                P.op("act", lambda e, ps=ps, t0=t0, n=n: e.activation(out=siluz[:, t0:t0 + n], in_=ps[:, 0:n], func=AF.Silu),
                                     reads=[psB], writes=[szB_])
                        feat_proj(wv, wB, q * 128, KC, lambda kc, t0, n: hT[:, kc, t0:t0 + n], hTB, ev)
                for g in range(3):
                    d = DIL[g]
                    nb = 16 // d
                    wv, wB = wget("att_w_in", la, 0, c.D, self.att_cols(hg, 2 + g))
                    for j in range(16):
                        r, cb = j // nb, j % nb
                        s0 = cb * 128 * d + r

                        def ev_kv(ps, psB, j=j, r=r, cb=cb, g=g):
                            nonlocal kv_i
                            P.op("act", lambda e, ps=ps, j=j: e.activation(out=KV[:, j, :], in_=ps[:, 0:256], func=AF.Copy), reads=[psB], writes=[KVB])
                            keep = (g == 2) or (g == 1 and cb == nb - 1) or (g == 0 and cb == 15)
                            if keep:
                                k = kv_i % 3; kv_i += 1
                                P.op("act", lambda e, ps=ps, k=k: e.activation(out=kvst[k][:, :], in_=ps[:, 0:256], func=AF.Copy), reads=[psB], writes=[kvstB[k]])
                                dst = O["kvp%d" % g][la, r:WIN[g]:d, :, ho, :] if g > 0 else O["kvp0"][la, :, :, ho, :]
                                P.dma("sp", lambda e, k=k, dst=dst: e.dma_start(out=dst, in_=kvst[k][:, :].rearrange("p (a q) -> p a q", a=2)),
                                      reads=[kvstB[k]], writes=[self.outB], owner=kvstB[k])
                        tok_proj(wv, wB, 0, 256, lambda kc, s0=s0, d=d: hT[:, kc, s0:s0 + 127 * d + 1:d], 128, ev_kv)
                    for s in range(NS):
                        def ev_kvs(ps, psB, s=s, g=g):
                            nonlocal kv_i
                            P.op("act", lambda e, ps=ps, s=s: e.activation(out=KVs[0:1, s, :], in_=ps[0:1, 0:256], func=AF.Copy), reads=[psB], writes=[KVsB])
                            k = kv_i % 3; kv_i += 1
                            P.op("act", lambda e, ps=ps, k=k: e.activation(out=kvst[k][0:1, :], in_=ps[0:1, 0:256], func=AF.Copy), reads=[psB], writes=[kvstB[k]])
                            P.dma("sp", lambda e, k=k, s=s, g=g: e.dma_start(out=O["kvs%d" % g][la, s:s + 1, :, ho, :],
                                                                             in_=kvst[k][0:1, :].rearrange("p (a q) -> p a q", a=2)),
                                  reads=[kvstB[k]], writes=[self.outB], owner=kvstB[k])
                        tok_proj(wv, wB, 0, 256, lambda kc, s=s: hT[:, kc, T + s:T + s + 1], 1, ev_kvs)
                    tpb = tpp[:, :].bitcast(BF16)
                    for j4 in range(0, 16, 4):
                        for jj in range(4):
                            P.op("pe", lambda e, j4=j4, jj=jj: e.transpose(tpb[:, jj * 128:(jj + 1) * 128], KV[:, j4 + jj, 0:128], idb[:]),
                                 reads=[KVB, cB], writes=[tppB], sig=(jj == 3))
                        P.op("dve", lambda e, j4=j4: e.tensor_copy(out=KTt[:, j4:j4 + 4, :], in_=tpb[:, 0:512].rearrange("p (a q) -> p a q", a=4)),
                             reads=[tppB], writes=[KTB])
                    for j in range(16):
                        r, cb = j // nb, j % nb
                        s0 = cb * 128 * d + r
                        chunks = ([j - 1] if cb > 0 else []) + [j]
                        nch = len(chunks)
                        sp_, spB = nxt("sps")
                        qv = QT[:, g, s0:s0 + 127 * d + 1:d]
                        for ci, jk in enumerate(chunks):
                            pos = (1 if (nch == 2 and ci == 0) else 0)
                            P.op("pe", lambda e, sp_=sp_, jk=jk, qv=qv, pos=pos: e.matmul(sp_[:, pos * 128:(pos + 1) * 128], lhsT=KTt[:, jk, :], rhs=qv, start=True, stop=True),
                                 reads=[KTB, QTB], writes=[spB], sig=(ci == nch - 1))
                        k = blk_i % 2; blk_i += 1
                        P.op("dve", lambda e, sp_=sp_, k=k, nch=nch, g=g: e.tensor_tensor(out=Ssb[k][:, 0:nch * 128], in0=sp_[:, 0:nch * 128], in1=biasT[:, g, 0:nch * 128], op=ALU.add),
                             reads=[spB, biasB], writes=[SsbB[k]])
                        P.op("act", lambda e, k=k, nch=nch: e.activation(out=PT[k][:, 0:nch * 128], in_=Ssb[k][:, 0:nch * 128], func=AF.Exp),
                             reads=[SsbB[k]], writes=[PTB[k]])
                        up, upB = nxt("ulp")
                        for which in range(2):
                            for ci, jk in enumerate(chunks):
                                pos = (1 if (nch == 2 and ci == 0) else 0)
                                lhs = KV[:, jk, 128:256] if which == 0 else onesb[:, :]
                                P.op("pe", lambda e, up=up, which=which, lhs=lhs, k=k, pos=pos, ci=ci, nch=nch: e.matmul(
                                    up[:, which * 128:(which + 1) * 128], lhsT=lhs, rhs=PT[k][:, pos * 128:(pos + 1) * 128],
                                    start=(ci == 0), stop=(ci == nch - 1)), reads=[KVB, PTB[k], cB], writes=[upB],
                                    sig=(which == 1 and ci == nch - 1))
                        av = acc[:, :, s0:s0 + 127 * d + 1:d]
                        uv = up[:, 0:256].rearrange("p (a q) -> p a q", a=2)
                        if g == 0:
                            P.op("act", lambda e, av=av, uv=uv: e.activation(out=av, in_=uv, func=AF.Copy), reads=[upB], writes=[accB])
                        else:
                            P.op("dve", lambda e, av=av, uv=uv: e.tensor_tensor(out=av, in0=av, in1=uv, op=ALU.add), reads=[upB, accB], writes=[accB])
                    for s in range(NS):
                        k = c_i % 2; c_i += 1
                        csrc = I["ck%d" % g][la, s, 0:WIN[g]:d, :, ho, :]
                        P.dma("sp", lambda e, k=k, csrc=csrc: e.dma_start(out=cch[k][:, :].rearrange("p (a q) -> p a q", a=2), in_=csrc),
                              writes=[cchB[k]], owner=cchB[k])
                        P.op("pool", lambda e, k=k: e.tensor_copy(out=cchb[k][:, :], in_=cch[k][:, :]), reads=[cchB[k]], writes=[cchbB[k]])
                        P.op("pe", lambda e, k=k: e.transpose(tpb[:, 0:128], cchb[k][:, 0:128], idb[:]), reads=[cchbB[k], cB], writes=[tppB])
                        P.op("dve", lambda e: e.tensor_copy(out=kcT[:, :], in_=tpb[:, 0:128]), reads=[tppB], writes=[kcTB])
                        qs = QT[:, g, T + s:T + s + 1]
                        sp_, spB = nxt("sps")
                        P.op("pe", lambda e, sp_=sp_, qs=qs: e.matmul(sp_[:, 0:1], lhsT=kcT[:, :], rhs=qs, start=True, stop=True), reads=[kcTB, QTB], writes=[spB])
                        P.op("pe", lambda e, sp_=sp_, s=s: e.matmul(sp_[:, 8:9], lhsT=KVs[0:1, s, 0:128], rhs=onesb[0:1, 0:1], start=True, stop=True),
                             reads=[KVsB, cB], writes=[spB])
                        P.op("dve", lambda e, sp_=sp_: e.tensor_copy(out=knT[:, 0:1], in_=sp_[:, 8:9]), reads=[spB], writes=[knTB])
                        P.op("pe", lambda e, sp_=sp_, qs=qs: e.matmul(sp_[0:1, 16:17], lhsT=knT[:, 0:1], rhs=qs, start=True, stop=True), reads=[knTB, QTB], writes=[spB])
                        P.op("act", lambda e, sp_=sp_, g=g: e.activation(out=pts[:, 0:1], in_=sp_[:, 0:1], func=AF.Exp, bias=biasS[:, g, ho:ho + 1], scale=1.0),
                             reads=[spB, vecB], writes=[ptsB])
                        P.op("act", lambda e, sp_=sp_, g=g: e.activation(out=pts[0:1, 1:2], in_=sp_[0:1, 16:17], func=AF.Exp, bias=t5s[0:1, g * HO + ho:g * HO + ho + 1], scale=1.0),
                             reads=[spB, vecB], writes=[ptsB])
                        up, upB = nxt("ulp")
                        for which in range(2):
                            lhs = cchb[k][:, 128:256] if which == 0 else onesb[:, :]
                            lhs1 = KVs[0:1, s, 128:256] if which == 0 else onesb[0:1, :]
                            P.op("pe", lambda e, up=up, which=which, lhs=lhs: e.matmul(up[:, which:which + 1], lhsT=lhs, rhs=pts[:, 0:1], start=True, stop=False),
                                 reads=[cchbB[k], ptsB, cB], writes=[upB], sig=False)
                            P.op("pe", lambda e, up=up, which=which, lhs1=lhs1: e.matmul(up[:, which:which + 1], lhsT=lhs1, rhs=pts[0:1, 1:2], start=False, stop=True),
                                 reads=[KVsB, ptsB, cB], writes=[upB], sig=(which == 1))
                        av = acc[:, :, T + s]
                        if g == 0:
                            P.op("act", lambda e, av=av, up=up: e.activation(out=av, in_=up[:, 0:2], func=AF.Copy), reads=[upB], writes=[accB])
                        else:
                            P.op("dve", lambda e, av=av, up=up: e.tensor_tensor(out=av, in0=av, in1=up[:, 0:2], op=ALU.add), reads=[upB, accB], writes=[accB])
                P.op("dve", lambda e: e.reciprocal(out=rl[:, :], in_=acc[:, 1, :]), reads=[accB], writes=[rlB])
                P.op("dve", lambda e: e.tensor_tensor(out=rl[:, :], in0=rl[:, :], in1=acc[:, 0, :], op=ALU.mult), reads=[accB, rlB], writes=[rlB])
                P.op("dve", lambda e: e.tensor_tensor(out=gst[:, :], in0=rl[:, :], in1=siluz[:, :], op=ALU.mult), reads=[rlB, szB_], writes=[gstB])
                P.dma("sp", lambda e, ho=ho: e.dma_start(out=S["gT"][ho, :, :], in_=gst[:, :]), reads=[gstB], writes=[self.gTB[ho]], owner=gstB)

        self.outB = B("outputs")
        self.heads = list(range(HO))
        self.pool_groups = list(range(c.GO))
        import os
        stop = int(os.environ.get("KSTOP", "999"))
        phases = [("setup", setup)]
        for i in range(c.DEPTH):
            phases.append(("adaln%d" % i, lambda i=i: adaln(i)))
            phases.append(("norm%d" % i, lambda i=i: norm_phase(i)))
            if i % 2 == 0:
                phases.append(("pool%d" % i, lambda i=i: pool_phase(i)))
                phases.append(("wout%d" % i, lambda i=i: wout_phase("pool_w_out", i // 2, c.KCp, 2 if c.KCp > KC else 1)))
            else:
                phases.append(("att%d" % i, lambda i=i: att_phase(i)))
                phases.append(("wout%d" % i, lambda i=i: wout_phase("att_w_out", i // 2, HO, 1)))
        phases.append(("final", lambda: norm_phase(c.DEPTH)))
        for pi, (nm, fn) in enumerate(phases):
            if pi >= stop:
                break
            fn()
        if not dry and stop < 999:
            print("phases run:", [p[0] for p in phases[:stop]], "nops", P.nops)
        if not dry:
            P.barrier()
            P.emit()

    def att_cols(self, hg, blk):
        c = self.c
        QW = 3 * c.D

        def col(kind, g):
            if kind == "z":
                return 3 * QW + hg * 128
            off = {"q": 0, "k": QW, "v": 2 * QW}[kind]
            return off + g * c.D + hg * 128
        if blk == 0:
            st = [col("q", 0), col("q", 1)]
        elif blk == 1:
            st = [col("q", 2), col("z", 0)]
        else:
            st = [col("k", blk - 2), col("v", blk - 2)]
        return np.concatenate([np.arange(s, s + 128) for s in st])


def make_consts():
    oh = np.zeros((3, 33, FP), np.float32)
    ohs = np.zeros((3, 32, 128), np.float32)
    for g, d in enumerate(DIL):
        for j in range(FP):
            rel = j - 127
            if 0 <= rel <= 128:
                oh[g, int(t5_bucket(rel * d)), j] = 1.0
            else:
                oh[g, 32, j] = NEG
        for j in range(128):
            ohs[g, int(t5_bucket((128 - j) * d)), j] = 1.0
    invc = np.tile((1.0 / np.arange(1, 17, dtype=np.float32))[None, :], (128, 1)).astype(np.float32)
    fmask = np.ascontiguousarray(np.tile(oh[None, :, 32, :], (128, 1, 1)))
    oh[:, 32, :] = 0.0
    return oh, ohs, invc, fmask


def fm(v):
    return np.ascontiguousarray(np.asarray(v, np.float32).reshape(-1, 128).T)


_CACHE = {}


def get_program(cfg):
    key = (cfg.D, cfg.NS, cfg.TP, cfg.DEPTH)
    if key in _CACHE:
        return _CACHE[key]
    bld = Builder(cfg)
    nc0 = bass.Bass("TRN2", target_bir_lowering=False)
    bld.declare(nc0, 1)
    with contextlib.ExitStack() as es:
        bld.build(nc0, es, dry=True)
    specs = bld.specs
    nc = bass.Bass("TRN2", target_bir_lowering=False)
    bld.declare(nc, len(specs))
    with contextlib.ExitStack() as es:
        bld.build(nc, es, dry=False)
    _CACHE[key] = (nc, specs)
    return nc, specs


def build_wstream(specs, W):
    ws = np.zeros((len(specs), 128, WSLOT), np.float32)
    for n, sp in enumerate(specs):
        src = W[sp["src"]]
        l = sp["layer"]
        m = src[l] if not isinstance(l, tuple) else src[l[0]][l[1]]
        blk = m[sp["r0"]:sp["r0"] + sp["nrows"]][:, sp["cols"]]
        blk = blk.reshape(sp["kcb"], 128, sp["C"]).transpose(1, 0, 2).reshape(128, -1)
        ws[n, :, :blk.shape[1]] = blk
    return ws


def run(cfg, inp):
    c = cfg
    nc, specs = get_program(cfg)
    BATCH = inp["x_prompt"].shape[0]
    DEC = inp["x_sample"].shape[0]
    assert c.TP == 1 and DEC == BATCH * c.NS
    W = {k: np.asarray(inp[k], np.float32) for k in ("ada_w", "pool_w_in", "pool_w_grp", "pool_w_out", "att_w_in", "att_w_out")}
    wstream = build_wstream(specs, W)
    oh, ohs, invc, fmask = make_consts()
    f32 = lambda a: np.ascontiguousarray(np.asarray(a, np.float32))
    adab = f32(np.stack([fm(inp["ada_b"][i]) for i in range(c.DEPTH)]))
    npre = f32(np.stack([fm(inp["norm_pre"][i]) for i in range(c.DEPTH)]))
    npost = f32(np.stack([fm(inp["norm_post"][i]) for i in range(c.DEPTH)]))
    pscale = f32(np.stack([fm(inp["pool_scale"][l]) for l in range(c.LP)]))
    t5o = f32(np.asarray(inp["t5_bias"], np.float32))
    in_maps = []
    n_cores = 8
    for core in range(n_cores):
        b = core % BATCH
        ss = slice(b * c.NS, (b + 1) * c.NS)
        cst = np.concatenate([np.asarray(inp["c_prompt"][b:b + 1], np.float32), np.asarray(inp["c_sample"][ss], np.float32)], axis=0)
        cT = f32(cst.reshape(1 + c.NS, c.KC, 128).transpose(2, 1, 0))
        m = dict(xp=f32(inp["x_prompt"][b]), xs=f32(np.asarray(inp["x_sample"])[ss, 0, :]), cT=cT, adab=adab, npre=npre, npost=npost,
                 t5o=t5o, oh=oh, ohs=ohs, fmask=fmask, invc=invc, pscale=pscale, spool=f32(np.asarray(inp["state_pool"])[:, ss]),
                 ck0=f32(np.asarray(inp["cache_kv0"])[:, ss]), ck1=f32(np.asarray(inp["cache_kv1"])[:, ss]),
                 ck2=f32(np.asarray(inp["cache_kv2"])[:, ss]), wstream=wstream)
        in_maps.append(m)
    res = run_bass_kernel_spmd(nc, in_maps, core_ids=list(range(n_cores)))
    R = res.results
    yp = np.stack([R[b]["yp"] for b in range(BATCH)])
    ys = np.concatenate([R[b]["ys"] for b in range(BATCH)])[:, None, :]
    kvp = [np.stack([R[b]["kvp%d" % g] for b in range(BATCH)], axis=1) for g in range(3)]
    kvs = [np.concatenate([R[b]["kvs%d" % g] for b in range(BATCH)], axis=1)[:, :, None] for g in range(3)]
    poolp = np.stack([R[b]["poolp"] for b in range(BATCH)], axis=1)
    pools = np.concatenate([R[b]["pools"] for b in range(BATCH)], axis=1)
    outs = (yp, ys, kvp[0], kvp[1], kvp[2], poolp, kvs[0], kvs[1], kvs[2], pools)
    return tuple(np.ascontiguousarray(o.astype(np.float32)) for o in outs)


def kernel(**inputs):
    cfg = Cfg(D=2048, NS=2, TP=1, DEPTH=4)
    return run(cfg, inputs)
```

```python
import contextlib
import numpy as np
import concourse.bass as bass
import concourse.mybir as mybir
from concourse.bass_utils import run_bass_kernel_spmd

F32 = mybir.dt.float32
BF16 = mybir.dt.bfloat16
AF = mybir.ActivationFunctionType
ALU = mybir.AluOpType
AX = mybir.AxisListType

T = 2048
PB = 15
NEG = -30000.0
DIL = (1, 4, 16)
WIN = (128, 512, 2048)
PWINS = (2, 4, 8, 16)
N_BUCKETS = 32
T5_MAX_DIST = 2048
RMS_EPS = 1e-6
WSLOT = 4096
FP = 384
FREP = 130
ENGS = ("pe", "act", "dve", "pool", "sp")


def t5_bucket(dist):
    dist = np.asarray(dist, dtype=np.int64)
    max_exact = N_BUCKETS // 2
    ratio = np.log(np.maximum(dist, 1) / max_exact) / np.log(T5_MAX_DIST / max_exact)
    large = np.minimum(max_exact + (ratio * (N_BUCKETS - max_exact)).astype(np.int64), N_BUCKETS - 1)
    return np.where(dist < max_exact, dist, large).astype(np.int32)


class Cfg:
    def __init__(self, D=2048, NS=2, TP=1, DEPTH=4):
        self.D, self.NS, self.TP, self.DEPTH = D, NS, TP, DEPTH
        self.KC = D // 128
        self.H = D // 128
        self.HO = self.H // TP
        self.TT = T + NS
        self.PW = 2 * D
        self.PG = self.PW // 4
        self.GC = self.PG // 128
        self.GO = 4 // TP
        self.PWo = self.PW // TP
        self.KCp = self.PWo // 128
        self.LP = (DEPTH + 1) // 2
        self.LA = DEPTH // 2


class Buf:
    __slots__ = ("name", "w", "r", "sem", "dcnt")

    def __init__(self, name):
        self.name, self.w, self.r, self.sem, self.dcnt = name, None, [], None, 0


class _Rec:
    def __init__(self):
        self.spec = None

    def __getattr__(self, name):
        def f(*a, **k):
            self.spec = (name, a, k)
            return self
        return f


def _record(fn):
    r = _Rec()
    fn(r)
    assert r.spec is not None
    return r.spec


class Prog:
    def __init__(self, nc, es):
        self.nc, self.es = nc, es
        self.streams = {e: [] for e in ENGS}
        self.esem = {e: es.enter_context(nc.semaphore("S_" + e)) for e in ENGS}
        self.cnt = {e: 0 for e in ENGS}
        self.seen = {e: {} for e in ENGS}
        self.dry = False
        self.semtab = {}
        self.nops = 0
        self.klog = False
        import os
        self.maxops = int(os.environ.get('KMAXOPS', '100000000'))

    def _bsem(self, b):
        ent = self.semtab.get(b.name)
        if ent is None:
            ent = [self.es.enter_context(self.nc.semaphore("D_" + b.name)), 0]
            self.semtab[b.name] = ent
        return ent

    def _need(self, eng, sem, val):
        if self.klog and self.seen[eng].get(id(sem), 0) < val:
            print("   WAIT", eng, [k for k, v in self.semtab.items() if v[0] is sem] or [e for e in ENGS if self.esem[e] is sem], val)
        if self.seen[eng].get(id(sem), 0) < val:
            self.seen[eng][id(sem)] = val
            self.streams[eng].append(("wait", sem, val))

    def _waits(self, eng, reads, writes, skip):
        need = {}

        def add(ev):
            if ev[0] is skip:
                return
            k = id(ev[0])
            if k not in need or need[k][1] < ev[1]:
                need[k] = ev
        for b in reads:
            if b.w is not None:
                add(b.w)
        for b in writes:
            if b.w is not None:
                add(b.w)
            for ev in b.r:
                add(ev)
        for ev in need.values():
            self._need(eng, *ev)

    def op(self, eng, fn, reads=(), writes=(), sig=True):
        if self.dry:
            return
        self.nops += 1
        if self.nops > self.maxops:
            return
        fn = _record(fn)
        if self.klog:
            print("OP", self.nops, eng, fn[0], [getattr(a, "shape", None) for a in fn[1]], {k: (getattr(v, "shape", v)) for k, v in fn[2].items() if k in ("out", "in_", "lhsT", "rhs")})
        skip = self.esem[eng] if eng == "pe" else None
        self._waits(eng, reads, writes, skip)
        if sig:
            self.cnt[eng] += 1
            ev = (self.esem[eng], self.cnt[eng])
            self.streams[eng].append(("op", fn, self.esem[eng], 1))
        else:
            ev = (self.esem[eng], self.cnt[eng] + 1)
            self.streams[eng].append(("op", fn, None, 0))
        for b in writes:
            b.w, b.r = ev, []
        for b in reads:
            b.r.append(ev)

    def dma(self, q, fn, reads=(), writes=(), owner=None):
        if self.dry:
            return
        self.nops += 1
        if self.nops > self.maxops:
            return
        fn = _record(fn)
        if self.klog:
            print("OP", self.nops, q, "dma", {k: (getattr(v, "shape", v)) for k, v in fn[2].items() if k in ("out", "in_")})
        self._waits(q, reads, writes, None)
        ent = self._bsem(owner)
        ent[1] += 1
        sem = ent[0]
        ev = (sem, 16 * ent[1])
        self.streams[q].append(("op", fn, sem, 16))
        for b in writes:
            b.w, b.r = ev, []
        for b in reads:
            b.r.append(ev)

    def barrier(self):
        if self.dry:
            return
        for e in ENGS:
            for f in ("pe", "act", "dve", "pool"):
                if f != e and self.cnt[f] > 0:
                    self._need(e, self.esem[f], self.cnt[f])
            for ent in self.semtab.values():
                if ent[1] > 0:
                    self._need(e, ent[0], 16 * ent[1])

    def emit(self):
        nc, streams = self.nc, self.streams

        def replay(engobj, items):
            for it in items:
                if it[0] == "wait":
                    engobj.wait_ge(it[1], it[2])
                else:
                    ins = getattr(engobj, it[1][0])(*it[1][1], **it[1][2])
                    if it[2] is not None:
                        ins.then_inc(it[2], it[3])

        with nc.allow_non_contiguous_dma(reason="single sample-token columns / strided rows"), nc.Block() as block:
            @block.tensor
            def _(e):
                replay(e, streams["pe"])

            @block.scalar
            def _(e):
                replay(e, streams["act"])

            @block.vector
            def _(e):
                replay(e, streams["dve"])

            @block.gpsimd
            def _(e):
                replay(e, streams["pool"])

            @block.sync
            def _(e):
                replay(e, streams["sp"])


class Builder:
    def __init__(self, cfg):
        self.c = cfg
        self.specs = None

    def declare(self, nc, nblk):
        c = self.c
        di = lambda n, s, d=F32: nc.dram_tensor(n, list(s), d, kind="ExternalInput").ap()
        do = lambda n, s, d=F32: nc.dram_tensor(n, list(s), d, kind="ExternalOutput").ap()
        dn = lambda n, s, d=F32: nc.dram_tensor(n, list(s), d).ap()
        I = {}
        I["xp"] = di("xp", (T, c.D))
        I["xs"] = di("xs", (c.NS, c.D))
        I["cT"] = di("cT", (128, c.KC, 1 + c.NS))
        I["adab"] = di("adab", (c.DEPTH, 128, 3 * c.KC))
        I["npre"] = di("npre", (c.DEPTH, 128, c.KC))
        I["npost"] = di("npost", (c.DEPTH, 128, c.KC))
        I["t5o"] = di("t5o", (32, 3 * c.HO))
        I["oh"] = di("oh", (3, 33, FP))
        I["ohs"] = di("ohs", (3, 32, 128))
        I["fmask"] = di("fmask", (128, 3, FP))
        I["invc"] = di("invc", (128, 16))
        I["pscale"] = di("pscale", (c.LP, 128, c.KCp))
        I["spool"] = di("spool", (c.LP, c.NS, PB, c.PWo))
        for g in range(3):
            I["ck%d" % g] = di("ck%d" % g, (c.LA, c.NS, WIN[g], 2, c.HO, 128))
        I["wstream"] = di("wstream", (nblk, 128, WSLOT))
        O = {}
        O["yp"] = do("yp", (T, c.D))
        O["ys"] = do("ys", (c.NS, c.D))
        for g in range(3):
            O["kvp%d" % g] = do("kvp%d" % g, (c.LA, WIN[g], 2, c.HO, 128))
            O["kvs%d" % g] = do("kvs%d" % g, (c.LA, c.NS, 2, c.HO, 128))
        O["poolp"] = do("poolp", (c.LP, PB, c.PWo))
        O["pools"] = do("pools", (c.LP, c.NS, PB, c.PWo))
        S = {}
        S["xT"] = dn("xT", (c.KC, 128, c.TT))
        S["yT"] = dn("yT", (c.KC, 128, c.TT))
        S["gT"] = dn("gT", (max(c.KCp, c.HO), 128, c.TT), BF16)
        S["fd"] = dn("fd", (3 * c.HO, FREP * FP))
        self.I, self.O, self.S = I, O, S

    def build(self, nc, es, dry):
        c = self.c
        P = Prog(nc, es)
        P.dry = dry
        self.P, self.nc = P, nc
        if dry:
            self.specs = []
        self.wi = 0
        self.wseq = 0
        I, O, S = self.I, self.O, self.S
        KC, NS, TT, HO = c.KC, c.NS, c.TT, c.HO
        sbt = lambda n, s, d: es.enter_context(nc.sbuf_tensor("s_" + n, list(s), d))
        pst = lambda n, s, d: es.enter_context(nc.psum_tensor("p_" + n, list(s), d))
        B = lambda n: Buf(n)

        hraw = sbt("hraw", (128, KC * TT + 128), BF16); hTB = B("hT")
        hT = hraw[:, 0:KC * TT].rearrange("p (k t) -> p k t", k=KC)
        stage = [sbt("wst%d" % i, (128, WSLOT), F32) for i in range(2)]
        stB = [B("wst%d" % i) for i in range(2)]
        NWB = 3
        wbf = [sbt("wbf%d" % i, (128, WSLOT), BF16) for i in range(NWB)]
        wbB = [B("wbf%d" % i) for i in range(NWB)]
        idf = sbt("idf", (128, 128), F32); idb = sbt("idb", (128, 128), BF16)
        onesb = sbt("onesb", (128, 128), BF16); onesD = sbt("onesD", (128, 128), F32)
        onesf = sbt("onesf", (128, 128), F32)
        epsb = sbt("epsb", (128, 1), F32)
        cB = B("consts")
        cTs = sbt("cTs", (128, KC, 1 + NS), F32); scT = sbt("scT", (128, KC, 1 + NS), BF16); scB = B("scT")
        adab = sbt("adabs", (128, c.DEPTH, 3 * KC), F32)
        npre = sbt("npres", (128, c.DEPTH, KC), F32)
        npost = sbt("nposts", (128, c.DEPTH, KC), F32)
        vecB = B("vecs")
        modT = [sbt("modT%d" % i, (128, 3 * KC, 1 + NS), F32) for i in range(c.DEPTH)]
        modB = [B("modT%d" % i) for i in range(c.DEPTH)]
        tabA = [sbt("tabA%d" % i, (128, KC, 1 + NS), F32) for i in range(c.DEPTH)]
        tabG = [sbt("tabG%d" % i, (128, KC, 1 + NS), F32) for i in range(c.DEPTH)]
        tabB = [B("tab%d" % i) for i in range(c.DEPTH)]
        t5s = sbt("t5s", (32, 3 * HO), F32)
        biasS = sbt("biasS", (128, 3, HO), F32)
        invc = sbt("invc", (128, 16), F32)
        pscale = sbt("pscales", (128, c.LP, c.KCp), F32)
        ARENA = 19200
        arena = sbt("arena", (128, ARENA), F32)
        self.aoff = 0

        def areset():
            self.aoff = 0

        def aview(shape, dt):
            n = int(np.prod(shape))
            nel = n if dt == F32 else (n + 1) // 2
            off = self.aoff
            self.aoff += nel
            assert self.aoff <= ARENA, ("arena overflow", self.aoff)
            v = arena[:, off:off + nel]
            if dt == BF16:
                v = v.bitcast(BF16)[:, 0:n]
            if len(shape) == 2:
                v = v.rearrange("p (a b) -> p a b", a=shape[0])
            elif len(shape) == 3:
                v = v.rearrange("p (a b c) -> p a b c", a=shape[0], b=shape[1])
            return v

        banks = [pst("bk%d" % i, (128, 512), F32) for i in range(8)]
        bankB = [B("bk%d" % i) for i in range(8)]
        bankmap = {"pj": list(range(7)), "sps": [], "ulp": [], "tp": [7]}
        rot = {"pj": 0, "sps": 0, "ulp": 0, "tp": 0}

        def set_banks(att):
            if att:
                bankmap.update({"pj": [0, 1], "sps": [2, 3, 4], "ulp": [5, 6, 7], "tp": [0, 1]})
            else:
                bankmap.update({"pj": list(range(7)), "sps": [], "ulp": [], "tp": [7]})

        def nxt(kind):
            lst = bankmap[kind]
            key = "pj" if (kind == "tp" and bankmap["tp"] == bankmap["pj"]) else kind
            i = lst[rot[key] % len(lst)]
            rot[key] += 1
            return banks[i], bankB[i]

        ptiles = [(t0, 512) for t0 in range(0, T, 512)]
        alltiles = ptiles + [(T, NS)]

        def wload(n):
            sp = self.specs[n]
            ne = sp["kcb"] * sp["C"]
            s = n % 2
            P.dma("sp", lambda e, s=s, n=n, ne=ne: e.dma_start(out=stage[s][:, 0:ne], in_=I["wstream"][n, :, 0:ne]),
                  writes=[stB[s]], owner=stB[s])

        def wcast(n):
            sp = self.specs[n]
            ne = sp["kcb"] * sp["C"]
            s, d = n % 2, n % NWB
            P.op("pool", lambda e, s=s, d=d, ne=ne: e.tensor_copy(out=wbf[d][:, 0:ne], in_=stage[s][:, 0:ne]),
                 reads=[stB[s]], writes=[wbB[d]])

        def wget(src, layer, r0, nrows, cols):
            kcb, C = nrows // 128, len(cols)
            assert kcb * C <= WSLOT
            if dry:
                self.specs.append(dict(src=src, layer=layer, r0=r0, nrows=nrows, cols=np.asarray(cols), kcb=kcb, C=C))
                n = len(self.specs) - 1
            else:
                n = self.wi
                self.wi += 1
                nb_ = len(self.specs)
                seq_end = 2 * n + 4
                while self.wseq <= seq_end:
                    q = self.wseq
                    self.wseq += 1
                    if q < 2:
                        if q < nb_:
                            wload(q)
                    elif q % 2 == 0:
                        k = (q - 2) // 2
                        if k < nb_:
                            wcast(k)
                    else:
                        k = (q - 3) // 2 + 2
                        if k < nb_:
                            wload(k)
            d = n % NWB
            return wbf[d][:, 0:kcb * C].rearrange("p (k c) -> p k c", k=kcb), wbB[d]

        def setup():
            P.op("pool", lambda e: e.memset(idf[:], 0.0), writes=[cB])
            P.op("pool", lambda e: e.affine_select(out=idf[:], in_=idf[:], pattern=[[-1, 128]], compare_op=ALU.not_equal,
                                                   fill=1.0, base=0, channel_multiplier=1), reads=[cB], writes=[cB])
            P.op("pool", lambda e: e.tensor_copy(out=idb[:], in_=idf[:]), reads=[cB], writes=[cB])
            P.op("pool", lambda e: e.memset(onesb[:], 1.0), writes=[cB])
            P.op("pool", lambda e: e.memset(onesf[:], 1.0), writes=[cB])
            P.op("pool", lambda e: e.memset(onesD[:], 1.0 / c.D), writes=[cB])
            P.op("pool", lambda e: e.memset(epsb[:], RMS_EPS), writes=[cB])
            P.dma("sp", lambda e: e.dma_start(out=cTs[:], in_=I["cT"]), writes=[scB], owner=scB)
            P.op("act", lambda e: e.activation(out=scT[:], in_=cTs[:], func=AF.Silu), reads=[scB], writes=[scB])
            P.dma("sp", lambda e: e.dma_start(out=adab[:], in_=I["adab"].rearrange("l p k -> p l k")), writes=[vecB], owner=vecB)
            P.dma("sp", lambda e: e.dma_start(out=npre[:], in_=I["npre"].rearrange("l p k -> p l k")), writes=[vecB], owner=vecB)
            P.dma("sp", lambda e: e.dma_start(out=npost[:], in_=I["npost"].rearrange("l p k -> p l k")), writes=[vecB], owner=vecB)
            P.dma("sp", lambda e: e.dma_start(out=pscale[:], in_=I["pscale"].rearrange("l p k -> p l k")), writes=[vecB], owner=vecB)
            P.dma("sp", lambda e: e.dma_start(out=invc[:], in_=I["invc"]), writes=[vecB], owner=vecB)
            P.dma("sp", lambda e: e.dma_start(out=t5s[:], in_=I["t5o"]), writes=[vecB], owner=vecB)
            areset()
            ohsb = aview((3, FP), F32)
            ohss = aview((3, 128), F32)
            fsb = aview((3, FP), F32)[0:HO]
            fmk = aview((3, FP), F32)
            tB = B("t5tmp")
            P.dma("sp", lambda e: e.dma_start(out=ohsb[0:33], in_=I["oh"].rearrange("g b j -> b g j")), writes=[tB], owner=tB)
            P.dma("sp", lambda e: e.dma_start(out=ohss[0:32], in_=I["ohs"].rearrange("g b j -> b g j")), writes=[tB], owner=tB)
            P.dma("sp", lambda e: e.dma_start(out=fmk, in_=I["fmask"]), writes=[tB], owner=tB)
            fB = B("fsb")
            for g in range(3):
                ps, psB = nxt("pj")
                P.op("pe", lambda e, ps=ps, g=g: e.matmul(ps[0:HO, 0:FP], lhsT=t5s[:, g * HO:(g + 1) * HO], rhs=ohsb[0:32, g, :],
                                                           start=True, stop=True), reads=[vecB, tB], writes=[psB])
                P.op("dve", lambda e, ps=ps, g=g: e.tensor_tensor(out=fsb[:, g, :], in0=ps[0:HO, 0:FP], in1=fmk[0:HO, g, :], op=ALU.add),
                     reads=[psB, tB], writes=[fB])
                ps2, ps2B = nxt("pj")
                P.op("pe", lambda e, ps2=ps2, g=g: e.matmul(ps2[:, 0:HO], lhsT=ohss[0:32, g, :], rhs=t5s[:, g * HO:(g + 1) * HO],
                                                             start=True, stop=True), reads=[vecB, tB], writes=[ps2B])
                P.op("dve", lambda e, ps2=ps2, g=g: e.tensor_copy(out=biasS[:, g, :], in_=ps2[:, 0:HO]), reads=[ps2B], writes=[vecB])
            self.fdB = B("fd")
            import os
            for g in range(3):
                if os.environ.get("KSKIP_FD"):
                    break
                src = bass.AP(fsb.tensor, fsb[:, g, :].offset, [list(fsb.ap[0]), [0, FREP], [1, FP]])
                dst = S["fd"][g * HO:(g + 1) * HO, :].rearrange("h (r j) -> h r j", r=FREP)
                P.dma("sp", lambda e, src=src, dst=dst: e.dma_start(out=dst, in_=src), reads=[fB], writes=[self.fdB], owner=fB)

        def adaln(i):
            nb = 3 * c.D // 256
            for b in range(nb):
                wv, wB = wget("ada_w", i, 0, c.D, np.arange(b * 256, (b + 1) * 256))
                for m in range(2):
                    mc = b * 2 + m
                    ps, psB = nxt("pj")
                    for kc in range(KC):
                        P.op("pe", lambda e, ps=ps, wv=wv, kc=kc, m=m: e.matmul(
                            ps[:, 0:1 + NS], lhsT=wv[:, kc, m * 128:(m + 1) * 128], rhs=scT[:, kc, :],
                            start=(kc == 0), stop=(kc == KC - 1)), reads=[wB, scB], writes=[psB], sig=(kc == KC - 1))
                    P.op("dve", lambda e, ps=ps, mc=mc: e.tensor_scalar(
                        out=modT[i][:, mc, :], in0=ps[:, 0:1 + NS], scalar1=adab[:, i, mc:mc + 1], scalar2=0.0, op0=ALU.add, op1=ALU.add),
                        reads=[psB, vecB], writes=[modB[i]])
            a_ = npre[:, i, :]; b_ = npost[:, i, :]
            npb = bass.AP(a_.tensor, a_.offset, [list(a_.ap[0]), [1, KC], [0, 1 + NS]])
            npo = bass.AP(b_.tensor, b_.offset, [list(b_.ap[0]), [1, KC], [0, 1 + NS]])
            P.op("dve", lambda e: e.scalar_tensor_tensor(out=tabA[i][:], in0=modT[i][:, KC:2 * KC, :], scalar=1.0, in1=npb,
                                                         op0=ALU.add, op1=ALU.mult), reads=[modB[i], vecB], writes=[tabB[i]])
            P.op("dve", lambda e: e.tensor_tensor(out=tabG[i][:], in0=modT[i][:, 2 * KC:3 * KC, :], in1=npo, op=ALU.mult),
                 reads=[modB[i], vecB], writes=[tabB[i]])

        xTB = {}
        yTB = {}

        def dB(dct, key, nm):
            if key not in dct:
                dct[key] = B("%s_%s" % (nm, key))
            return dct[key]

        def rms_stats(src, n, sq, red, rstd, srcB, tmpB):
            P.op("act", lambda e: e.activation(out=sq[:, :, 0:n], in_=src[:, :, 0:n], func=AF.Square), reads=[srcB], writes=[tmpB])
            P.op("dve", lambda e: e.tensor_reduce(out=red[:, 0:n], in_=sq[:, :, 0:n].rearrange("p k t -> p t k"),
                                                  axis=AX.X, op=ALU.add), reads=[tmpB], writes=[tmpB])
            ps, psB = nxt("pj")
            P.op("pe", lambda e: e.matmul(ps[:, 0:n], lhsT=onesD[:], rhs=red[:, 0:n], start=True, stop=True),
                 reads=[tmpB, cB], writes=[psB])
            P.op("act", lambda e: e.activation(out=rstd[:, 0:n], in_=ps[:, 0:n], func=AF.Sqrt, bias=epsb[:, 0:1], scale=1.0),
                 reads=[psB, cB], writes=[tmpB])
            P.op("dve", lambda e: e.reciprocal(out=rstd[:, 0:n], in_=rstd[:, 0:n]), reads=[tmpB], writes=[tmpB])

        def bc_k(v, n):
            return bass.AP(v.tensor, v.offset, [list(v.ap[0]), [0, KC], [1, n]])

        def norm_phase(i):
            P.barrier()
            areset()
            NT = 256
            xt = [aview((KC, NT), F32) for _ in range(2)]; xtB = [B("xt0"), B("xt1")]
            yt = aview((KC, NT), F32); ytB = B("yt")
            sq = aview((KC, NT), F32)
            red = aview((1, NT), F32)[:, 0, :]
            rstd = aview((1, NT), F32)[:, 0, :]
            tmpB = B("nrm_tmp")
            xin = aview((1, c.D), F32)[:, 0, :] if i in (0, c.DEPTH) else None
            xinB = B("xin")
            tiles = [(t0, NT, 0) for t0 in range(0, T, NT)] + [(T + s, 1, 1 + s) for s in range(NS)]
            for ti, (t0, n, r) in enumerate(tiles):
                x_, xB_ = xt[ti % 2], xtB[ti % 2]
                if i == 0:
                    nsub = (n + 127) // 128
                    for sub in range(nsub):
                        m = min(128, n - sub * 128)
                        srcrows = I["xp"][t0 + sub * 128:t0 + sub * 128 + m, :] if t0 < T else I["xs"][t0 - T:t0 - T + 1, :]
                        P.dma("sp", lambda e, srcrows=srcrows, m=m: e.dma_start(out=xin[0:m, :], in_=srcrows), writes=[xinB], owner=xinB)
                        for k4 in range(0, KC, 4):
                            nk = min(4, KC - k4)
                            tpp, tppB = nxt("tp")
                            for kk in range(nk):
                                P.op("pe", lambda e, kk=kk, k4=k4, m=m: e.transpose(tpp[:, kk * 128:kk * 128 + m], xin[0:m, (k4 + kk) * 128:(k4 + kk + 1) * 128], idf[0:m, 0:m]),
                                     reads=[xinB, cB], writes=[tppB], sig=(kk == nk - 1))
                            P.op("dve", lambda e, x_=x_, k4=k4, nk=nk, m=m, sub=sub: e.tensor_copy(
                                out=x_[:, k4:k4 + nk, sub * 128:sub * 128 + m],
                                in_=tpp[:, 0:nk * 128].rearrange("p (k t) -> p k t", k=nk)[:, :, 0:m]), reads=[tppB], writes=[xB_])
                else:
                    yb = dB(yTB, (t0 // 512) if t0 < T else 4 + (t0 - T), "yT"); xb = dB(xTB, ti, "xT")
                    P.dma("sp", lambda e, t0=t0, n=n: e.dma_start(out=yt[:, :, 0:n], in_=S["yT"][:, :, t0:t0 + n].rearrange("k p t -> p k t")),
                          reads=[yb], writes=[ytB], owner=ytB)
                    P.dma("sp", lambda e, t0=t0, n=n, x_=x_: e.dma_start(out=x_[:, :, 0:n], in_=S["xT"][:, :, t0:t0 + n].rearrange("k p t -> p k t")),
                          reads=[xb], writes=[xB_], owner=xB_)
                    rms_stats(yt, n, sq, red, rstd, ytB, tmpB)
                    P.op("dve", lambda e, n=n: e.tensor_tensor(out=yt[:, :, 0:n], in0=yt[:, :, 0:n], in1=bc_k(rstd[:, 0:n], n), op=ALU.mult),
                         reads=[ytB, tmpB], writes=[ytB])
                    for kc in range(KC):
                        P.op("dve", lambda e, kc=kc, n=n, x_=x_, r=r: e.scalar_tensor_tensor(
                            out=x_[:, kc, 0:n], in0=yt[:, kc, 0:n], scalar=tabG[i - 1][:, kc, r:r + 1], in1=x_[:, kc, 0:n],
                            op0=ALU.mult, op1=ALU.add), reads=[ytB, xB_, tabB[i - 1]], writes=[xB_])
                if i < c.DEPTH:
                    xb = dB(xTB, ti, "xT")
                    P.dma("sp", lambda e, t0=t0, n=n, x_=x_: e.dma_start(out=S["xT"][:, :, t0:t0 + n].rearrange("k p t -> p k t"), in_=x_[:, :, 0:n]),
                          reads=[xB_], writes=[xb], owner=xB_)
                    rms_stats(x_, n, sq, red, rstd, xB_, tmpB)
                    P.op("dve", lambda e, n=n, x_=x_: e.tensor_tensor(out=yt[:, :, 0:n], in0=x_[:, :, 0:n], in1=bc_k(rstd[:, 0:n], n), op=ALU.mult),
                         reads=[xB_, tmpB], writes=[ytB])
                    for kc in range(KC):
                        P.op("act", lambda e, kc=kc, n=n, t0=t0, r=r: e.activation(
                            out=hT[:, kc, t0:t0 + n], in_=yt[:, kc, 0:n], func=AF.Identity,
                            scale=tabA[i][:, kc, r:r + 1], bias=modT[i][:, kc, r:r + 1]), reads=[ytB, tabB[i], modB[i]], writes=[hTB])
                else:
                    nsub = (n + 127) // 128
                    for sub in range(nsub):
                        m = min(128, n - sub * 128)
                        for k4 in range(0, KC, 4):
                            nk = min(4, KC - k4)
                            tpp, tppB = nxt("tp")
                            for kk in range(nk):
                                P.op("pe", lambda e, kk=kk, k4=k4, m=m, sub=sub, x_=x_: e.transpose(
                                    tpp[0:m, kk * 128:(kk + 1) * 128], x_[:, k4 + kk, sub * 128:sub * 128 + m], idf[:]),
                                    reads=[xB_, cB], writes=[tppB], sig=(kk == nk - 1))
                            P.op("dve", lambda e, k4=k4, nk=nk, m=m: e.tensor_copy(out=xin[0:m, k4 * 128:(k4 + nk) * 128], in_=tpp[0:m, 0:nk * 128]),
                                 reads=[tppB], writes=[xinB])
                        dst = O["yp"][t0 + sub * 128:t0 + sub * 128 + m, :] if t0 < T else O["ys"][t0 - T:t0 - T + 1, :]
                        P.dma("sp", lambda e, dst=dst, m=m: e.dma_start(out=dst, in_=xin[0:m, :]), reads=[xinB], writes=[], owner=xinB)

        def feat_proj(wv, wB, coff, kcn, rhs_fn, rhsB, evac, tiles=alltiles):
            for (t0, n) in tiles:
                ps, psB = nxt("pj")
                for kc in range(kcn):
                    P.op("pe", lambda e, ps=ps, kc=kc, t0=t0, n=n: e.matmul(
                        ps[:, 0:n], lhsT=wv[:, kc, coff:coff + 128], rhs=rhs_fn(kc, t0, n), start=(kc == 0), stop=(kc == kcn - 1)),
                        reads=[wB, rhsB], writes=[psB], sig=(kc == kcn - 1))
                evac(ps, psB, t0, n)

        def tok_proj(wv, wB, c0, ncols, colsel, m, evac):
            ps, psB = nxt("pj")
            for kc in range(KC):
                P.op("pe", lambda e, ps=ps, kc=kc: e.matmul(ps[0:m, 0:ncols], lhsT=colsel(kc), rhs=wv[:, kc, c0:c0 + ncols],
                                                            start=(kc == 0), stop=(kc == KC - 1)),
                     reads=[wB, hTB], writes=[psB], sig=(kc == KC - 1))
            evac(ps, psB)

        def wout_phase(src, li, kcg, npass):
            P.barrier()
            set_banks(False)
            areset()
            yst = [aview((1, 512), F32)[:, 0, :] for _ in range(3)]; ystB = [B("yst%d" % k) for k in range(3)]
            Cw = WSLOT // (kcg * 128) * 128
            Cw = min(Cw, 256)
            nbw = c.D // Cw
            cnt = 0
            for ps_i in range(npass):
                if npass == 1:
                    tl = alltiles; c0, c1 = 0, TT
                elif ps_i == 0:
                    tl = ptiles[0:2]; c0, c1 = 0, 1024
                else:
                    tl = ptiles[2:4] + [(T, NS)]; c0, c1 = 1024, TT
                w = c1 - c0
                assert kcg * w <= KC * TT + 128
                gin = hraw[:, 0:kcg * w].rearrange("p (k t) -> p k t", k=kcg)
                for k in range(kcg):
                    P.dma("sp", lambda e, k=k, c0=c0, c1=c1, gin=gin: e.dma_start(out=gin[:, k, :], in_=S["gT"][k, :, c0:c1]),
                          reads=[self.gTB[k]], writes=[hTB], owner=hTB)
                for b in range(nbw):
                    wv, wB = wget(src, li, 0, kcg * 128, np.arange(b * Cw, (b + 1) * Cw))
                    for m in range(Cw // 128):
                        mc = b * (Cw // 128) + m

                        def ev(ps, psB, t0, n, mc=mc):
                            nonlocal cnt
                            k = cnt % 3; cnt += 1
                            if cnt % 2:
                                P.op("act", lambda e: e.activation(out=yst[k][:, 0:n], in_=ps[:, 0:n], func=AF.Copy), reads=[psB], writes=[ystB[k]])
                            else:
                                P.op("dve", lambda e: e.tensor_copy(out=yst[k][:, 0:n], in_=ps[:, 0:n]), reads=[psB], writes=[ystB[k]])
                            ti = (t0 // 512) if t0 < T else None
                            dsts = [dB(yTB, ti, "yT")] if ti is not None else [dB(yTB, 4 + s, "yT") for s in range(NS)]
                            P.dma("sp", lambda e, k=k, t0=t0, n=n, mc=mc: e.dma_start(out=S["yT"][mc, :, t0:t0 + n], in_=yst[k][:, 0:n]),
                                  reads=[ystB[k]], writes=dsts, owner=ystB[k])
                        feat_proj(wv, wB, m * 128, kcg, lambda kc, t0, n, gin=gin, c0=c0: gin[:, kc, t0 - c0:t0 - c0 + n], hTB, ev, tiles=tl)

        def pool_phase(i):
            li = i // 2
            P.barrier()
            areset()
            TP16 = 16 + T
            ub = aview((1, TP16), F32)[:, 0, :]; ubB = B("ub")
            sa = aview((1, TP16), F32)[:, 0, :]; saB = B("sa")
            sb_ = aview((1, TP16), F32)[:, 0, :]; sbB = B("sb")
            rT = aview((c.GC, TT), BF16); rTB = B("rT")
            sz = [aview((1, 512), F32)[:, 0, :] for _ in range(2)]; szB = [B("sz0"), B("sz1")]
            gst = [aview((1, TT), BF16)[:, 0, :] for _ in range(2)]; gstB = [B("gst0"), B("gst1")]
            ust = [aview((1, 128), F32)[:, 0, :] for _ in range(2)]; ustB = [B("ust0"), B("ust1")]
            stt = aview((1, 128), F32)[:, 0, :]; sttB = B("stt")
            uext = aview((NS, 16), F32); uextB = B("uext")
            rs_ = aview((1, NS), F32)[:, 0, :]
            P.op("pool", lambda e: e.memset(ub[:, 0:16], 0.0), writes=[ubB])
            P.op("pool", lambda e: e.memset(sa[:, 0:16], 0.0), writes=[saB])
            P.op("pool", lambda e: e.memset(sb_[:, 0:16], 0.0), writes=[sbB])
            cpB = B("poolcp")
            for s in range(NS):
                P.dma("sp", lambda e, s=s: e.dma_start(out=O["pools"][li, s, 0:PB - 1, :], in_=I["spool"][li, s, 1:PB, :]),
                      writes=[], owner=cpB)
            self.gTB = [B("gT%d" % k) for k in range(c.KCp)]
            ucnt = 0
            self.zcnt = 0
            for go in range(c.GO):
                gw = self.pool_groups[go]
                w = PWINS[gw]
                for fc in range(c.GC):
                    fo = go * c.GC + fc
                    if fc % 2 == 0:
                        nch = min(2, c.GC - fc)
                        cols = np.concatenate([np.arange(gw * c.PG + (fc + q) * 128, gw * c.PG + (fc + q + 1) * 128) for q in range(nch)])
                        wv, wB = wget("pool_w_in", li, 0, c.D, cols)
                    coff = (fc % 2) * 128

                    def ev_u(ps, psB, t0, n):
                        if t0 < T:
                            P.op("act", lambda e, ps=ps, t0=t0, n=n: e.activation(out=ub[:, 16 + t0:16 + t0 + n], in_=ps[:, 0:n], func=AF.Copy),
                                 reads=[psB], writes=[ubB])
                        else:
                            for s in range(NS):
                                P.op("act", lambda e, ps=ps, s=s: e.activation(out=uext[:, s, 15:16], in_=ps[:, s:s + 1], func=AF.Copy),
                                     reads=[psB], writes=[uextB])
                    feat_proj(wv, wB, coff, KC, lambda kc, t0, n: hT[:, kc, t0:t0 + n], hTB, ev_u)
                    k = ucnt % 2; ucnt += 1

                    def ev_rows(ps, psB, k=k, fo=fo):
                        P.op("dve", lambda e, ps=ps, k=k: e.tensor_copy(out=ust[k][0:PB + NS, :], in_=ps[0:PB + NS, 0:128]), reads=[psB], writes=[ustB[k]])
                        P.dma("sp", lambda e, k=k, fo=fo: e.dma_start(out=O["poolp"][li, :, fo * 128:(fo + 1) * 128], in_=ust[k][0:PB, :]),
                              reads=[ustB[k]], writes=[], owner=ustB[k])
                        for s in range(NS):
                            P.dma("sp", lambda e, k=k, fo=fo, s=s: e.dma_start(out=O["pools"][li, s, PB - 1:PB, fo * 128:(fo + 1) * 128],
                                                                               in_=ust[k][PB + s:PB + s + 1, :]),
                                  reads=[ustB[k]], writes=[], owner=ustB[k])
                    tok_proj(wv, wB, coff, 128, lambda kc: hT[:, kc, T - PB:T + NS], PB + NS, ev_rows)
                    for s in range(NS):
                        P.dma("sp", lambda e, s=s, fo=fo: e.dma_start(out=stt[0:PB, :], in_=I["spool"][li, s, :, fo * 128:(fo + 1) * 128]),
                              writes=[sttB], owner=sttB)
                        tpp, tppB = nxt("tp")
                        P.op("pe", lambda e: e.transpose(tpp[:, 0:PB], stt[0:PB, :], idf[0:PB, 0:PB]), reads=[sttB, cB], writes=[tppB])
                        P.op("dve", lambda e, s=s: e.tensor_copy(out=uext[:, s, 0:PB], in_=tpp[:, 0:PB]), reads=[tppB], writes=[uextB])
                    cur, curB = ub, ubB
                    bufs = [(sa, saB), (sb_, sbB)]
                    sh = 1
                    bi = 0
                    while sh < w:
                        dst, dstB = bufs[bi % 2]; bi += 1
                        P.op("dve" if bi % 2 else "pool", lambda e, dst=dst, cur=cur, sh=sh: e.tensor_tensor(
                            out=dst[:, 16:TP16], in0=cur[:, 16:TP16], in1=cur[:, 16 - sh:TP16 - sh], op=ALU.add),
                            reads=[curB], writes=[dstB])
                        cur, curB = dst, dstB
                        sh *= 2
                    fin, finB = bufs[bi % 2]
                    P.op("dve", lambda e, fin=fin, cur=cur, w=w: e.scalar_tensor_tensor(
                        out=fin[:, 16:TP16], in0=cur[:, 16:TP16], scalar=1.0 / w, in1=ub[:, 16:TP16], op0=ALU.mult, op1=ALU.subtract),
                        reads=[curB, ubB], writes=[finB])
                    P.op("dve", lambda e, fin=fin, cur=cur, w=w: e.tensor_tensor(out=fin[:, 16:16 + w - 1], in0=cur[:, 16:16 + w - 1], in1=invc[:, 0:w - 1], op=ALU.mult),
                         reads=[curB, vecB], writes=[finB])
                    P.op("dve", lambda e, fin=fin, w=w: e.tensor_tensor(out=fin[:, 16:16 + w - 1], in0=fin[:, 16:16 + w - 1], in1=ub[:, 16:16 + w - 1], op=ALU.subtract),
                         reads=[ubB], writes=[finB])
                    P.op("act", lambda e, fin=fin, fc=fc: e.activation(out=rT[:, fc, 0:T], in_=fin[:, 16:TP16], func=AF.Copy), reads=[finB], writes=[rTB])
                    P.op("dve", lambda e, w=w: e.tensor_reduce(out=rs_[:, 0:NS], in_=uext[:, :, 16 - w:16], axis=AX.X, op=ALU.add), reads=[uextB], writes=[uextB])
                    P.op("dve", lambda e, w=w, fc=fc: e.scalar_tensor_tensor(out=rT[:, fc, T:TT], in0=rs_[:, 0:NS], scalar=1.0 / w, in1=uext[:, :, 15],
                                                                             op0=ALU.mult, op1=ALU.subtract), reads=[uextB], writes=[rTB])
                Cg = min(256, WSLOT // c.GC // 128 * 128, c.PG)
                for bg in range(c.PG // Cg):
                    wg, wgB = wget("pool_w_grp", (li, gw), 0, c.PG, np.arange(bg * Cg, (bg + 1) * Cg))
                    for zb in range(0, Cg // 128, 2):
                        nz = min(2, Cg // 128 - zb)
                        zc0 = c.PW + gw * c.PG + bg * Cg + zb * 128
                        wz, wzB = wget("pool_w_in", li, 0, c.D, np.arange(zc0, zc0 + nz * 128))
                        for q in range(nz):
                            m = zb + q
                            fo = go * c.GC + bg * (Cg // 128) + m
                            k = fo % 2
                            for (t0, n) in alltiles:
                                pa, paB = nxt("pj")
                                for kc in range(c.GC):
                                    P.op("pe", lambda e, pa=pa, kc=kc, m=m, t0=t0, n=n: e.matmul(
                                        pa[:, 0:n], lhsT=wg[:, kc, m * 128:(m + 1) * 128], rhs=rT[:, kc, t0:t0 + n],
                                        start=(kc == 0), stop=(kc == c.GC - 1)), reads=[wgB, rTB], writes=[paB], sig=(kc == c.GC - 1))
                                pz, pzB = nxt("pj")
                                for kc in range(KC):
                                    P.op("pe", lambda e, pz=pz, kc=kc, q=q, t0=t0, n=n: e.matmul(
                                        pz[:, 0:n], lhsT=wz[:, kc, q * 128:(q + 1) * 128], rhs=hT[:, kc, t0:t0 + n],
                                        start=(kc == 0), stop=(kc == KC - 1)), reads=[wzB, hTB], writes=[pzB], sig=(kc == KC - 1))
                                zi = self.zcnt % 2; self.zcnt += 1
                                P.op("act", lambda e, pz=pz, zi=zi, n=n: e.activation(out=sz[zi][:, 0:n], in_=pz[:, 0:n], func=AF.Silu),
                                     reads=[pzB], writes=[szB[zi]])
                                P.op("dve", lambda e, pa=pa, zi=zi, n=n, t0=t0, k=k, fo=fo: e.scalar_tensor_tensor(
                                    out=gst[k][:, t0:t0 + n], in0=pa[:, 0:n], scalar=pscale[:, li, fo:fo + 1], in1=sz[zi][:, 0:n],
                                    op0=ALU.mult, op1=ALU.mult), reads=[paB, szB[zi], vecB], writes=[gstB[k]])
                            P.dma("sp", lambda e, k=k, fo=fo: e.dma_start(out=S["gT"][fo, :, :], in_=gst[k][:, :]),
                                  reads=[gstB[k]], writes=[self.gTB[fo]], owner=gstB[k])

        def att_phase(i):
            la = i // 2
            P.barrier()
            set_banks(True)
            areset()
            QT = aview((3, TT), BF16); QTB = B("QT")
            siluz = aview((1, TT), BF16)[:, 0, :]; szB_ = B("siluz")
            acc = aview((2, TT), F32); accB = B("acc")
            KV = aview((16, 256), BF16); KVB = B("KV")
            KTt = aview((16, 128), BF16); KTB = B("KT")
            KVs = aview((NS, 256), BF16); KVsB = B("KVs")
            gst = aview((1, TT), BF16)[:, 0, :]; gstB = B("gst")
            biasT = aview((3, 256), F32); biasB = B("biasT")
            Ssb = [aview((1, 256), F32)[:, 0, :] for _ in range(4)]; SsbB = [B("Ssb%d" % k) for k in range(4)]
            PT = [aview((1, 256), BF16)[:, 0, :] for _ in range(4)]; PTB = [B("PT%d" % k) for k in range(4)]
            self.blk_i = 0
            kvst = [aview((1, 256), F32)[:, 0, :] for _ in range(3)]; kvstB = [B("kvst%d" % k) for k in range(3)]
            cch = [aview((1, 256), F32)[:, 0, :] for _ in range(2)]; cchB = [B("cch0"), B("cch1")]
            cchb = [aview((1, 256), BF16)[:, 0, :] for _ in range(2)]; cchbB = [B("cchb0"), B("cchb1")]
            kcT = aview((1, 128), BF16)[:, 0, :]; kcTB = B("kcT")
            knT = aview((1, 4), BF16)[:, 0, :]; knTB = B("knT")
            pts = aview((1, 4), BF16)[:, 0, :]; ptsB = B("pts")
            rl = aview((1, TT), F32)[:, 0, :]; rlB = B("rl")
            self.gTB = [B("gT%d" % k) for k in range(HO)]
            blk_i = 0
            kv_i = 0
            c_i = 0
            for ho in range(HO):
                hg = self.heads[ho]
                for g in range(3):
                    row = g * HO + ho
                    src = bass.AP(S["fd"].tensor, S["fd"][row, :].offset + 127, [[FP - 1, 128], [128, 2], [1, 128]])
                    P.dma("sp", lambda e, g=g, src=src: e.dma_start(out=biasT[:, g, :].rearrange("p (a q) -> p a q", a=2), in_=src),
                          reads=[self.fdB], writes=[biasB], owner=biasB)
                for (blk, items) in ((0, (("q", 0), ("q", 1))), (1, (("q", 2), ("z", 0)))):
                    wv, wB = wget("att_w_in", la, 0, c.D, self.att_cols(hg, blk))
                    for q, (kind, g) in enumerate(items):
                        if kind == "q":
                            def ev(ps, psB, t0, n, g=g):
                                P.op("act", lambda e, ps=ps, t0=t0, n=n, g=g: e.activation(out=QT[:, g, t0:t0 + n], in_=ps[:, 0:n], func=AF.Copy, scale=128 ** -0.5),
                                     reads=[psB], writes=[QTB])
                        else:
                            def ev(ps, psB, t0, n):
                                P.op("act", lambda e, ps=ps, t0=t0, n=n: e.activation(out=siluz[:, t0:t0 + n], in_=ps[:, 0:n], func=AF.Silu),
                                     reads=[psB], writes=[szB_])
                        feat_proj(wv, wB, q * 128, KC, lambda kc, t0, n: hT[:, kc, t0:t0 + n], hTB, ev)
                for g in range(3):
                    d = DIL[g]
                    nb = 16 // d
                    wv, wB = wget("att_w_in", la, 0, c.D, self.att_cols(hg, 2 + g))
                    for j in range(16):
                        r, cb = j // nb, j % nb
                        s0 = cb * 128 * d + r

                        def ev_kv(ps, psB, j=j, r=r, cb=cb, g=g):
                            nonlocal kv_i
                            P.op("act", lambda e, ps=ps, j=j: e.activation(out=KV[:, j, :], in_=ps[:, 0:256], func=AF.Copy), reads=[psB], writes=[KVB])
                            keep = (g == 2) or (g == 1 and cb == nb - 1) or (g == 0 and cb == 15)
                            if keep:
                                k = kv_i % 3; kv_i += 1
                                P.op("act", lambda e, ps=ps, k=k: e.activation(out=kvst[k][:, :], in_=ps[:, 0:256], func=AF.Copy), reads=[psB], writes=[kvstB[k]])
                                dst = O["kvp%d" % g][la, r:WIN[g]:d, :, ho, :] if g > 0 else O["kvp0"][la, :, :, ho, :]
                                P.dma("sp", lambda e, k=k, dst=dst: e.dma_start(out=dst, in_=kvst[k][:, :].rearrange("p (a q) -> p a q", a=2)),
                                      reads=[kvstB[k]], writes=[], owner=kvstB[k])
                        tok_proj(wv, wB, 0, 256, lambda kc, s0=s0, d=d: hT[:, kc, s0:s0 + 127 * d + 1:d], 128, ev_kv)
                    for s in range(NS):
                        def ev_kvs(ps, psB, s=s, g=g):
                            nonlocal kv_i
                            P.op("act", lambda e, ps=ps, s=s: e.activation(out=KVs[0:1, s, :], in_=ps[0:1, 0:256], func=AF.Copy), reads=[psB], writes=[KVsB])
                            k = kv_i % 3; kv_i += 1
                            P.op("act", lambda e, ps=ps, k=k: e.activation(out=kvst[k][0:1, :], in_=ps[0:1, 0:256], func=AF.Copy), reads=[psB], writes=[kvstB[k]])
                            P.dma("sp", lambda e, k=k, s=s, g=g: e.dma_start(out=O["kvs%d" % g][la, s:s + 1, :, ho, :],
                                                                             in_=kvst[k][0:1, :].rearrange("p (a q) -> p a q", a=2)),
                                  reads=[kvstB[k]], writes=[], owner=kvstB[k])
                        tok_proj(wv, wB, 0, 256, lambda kc, s=s: hT[:, kc, T + s:T + s + 1], 1, ev_kvs)
                    for j4 in range(0, 16, 4):
                        tpp, tppB = nxt("tp")
                        tpb = tpp[:, :].bitcast(BF16)
                        for jj in range(4):
                            P.op("pe", lambda e, j4=j4, jj=jj: e.transpose(tpb[:, jj * 128:(jj + 1) * 128], KV[:, j4 + jj, 0:128], idb[:]),
                                 reads=[KVB, cB], writes=[tppB], sig=(jj == 3))
                        P.op("dve", lambda e, j4=j4: e.tensor_copy(out=KTt[:, j4:j4 + 4, :], in_=tpb[:, 0:512].rearrange("p (a q) -> p a q", a=4)),
                             reads=[tppB], writes=[KTB])
                    st = {}

                    def stageA(j, g=g, d=d, nb=nb):
                        r, cb = j // nb, j % nb
                        s0 = cb * 128 * d + r
                        chunks = ([j - 1] if cb > 0 else []) + [j]
                        nch = len(chunks)
                        k = self.blk_i % 4; self.blk_i += 1
                        sp_, spB = nxt("sps")
                        up_, upB_ = nxt("ulp")
                        qv = QT[:, g, s0:s0 + 127 * d + 1:d]
                        for ci, jk in enumerate(chunks):
                            pos = (1 if (nch == 2 and ci == 0) else 0)
                            P.op("pe", lambda e: e.matmul(sp_[:, pos * 128:(pos + 1) * 128], lhsT=KTt[:, jk, :], rhs=qv, start=True, stop=True),
                                 reads=[KTB, QTB], writes=[spB], sig=(ci == nch - 1))
                        P.op("dve", lambda e: e.tensor_tensor(out=Ssb[k][:, 0:nch * 128], in0=sp_[:, 0:nch * 128], in1=biasT[:, g, 0:nch * 128], op=ALU.add),
                             reads=[spB, biasB], writes=[SsbB[k]])
                        P.op("act", lambda e: e.activation(out=PT[k][:, 0:nch * 128], in_=Ssb[k][:, 0:nch * 128], func=AF.Exp),
                             reads=[SsbB[k]], writes=[PTB[k]])
                        st[j] = (k, chunks, s0, up_, upB_)

                    def stageB(j, g=g, d=d):
                        k, chunks, s0, up, upB = st[j]
                        nch = len(chunks)
                        for which in range(2):
                            for ci, jk in enumerate(chunks):
                                pos = (1 if (nch == 2 and ci == 0) else 0)
                                lhs = KV[:, jk, 128:256] if which == 0 else onesb[:, :]
                                P.op("pe", lambda e: e.matmul(
                                    up[:, which * 128:(which + 1) * 128], lhsT=lhs, rhs=PT[k][:, pos * 128:(pos + 1) * 128],
                                    start=(ci == 0), stop=(ci == nch - 1)), reads=[KVB, PTB[k], cB], writes=[upB],
                                    sig=(which == 1 and ci == nch - 1))
                        av = acc[:, :, s0:s0 + 127 * d + 1:d]
                        uv = up[:, 0:256].rearrange("p (a q) -> p a q", a=2)
                        if g == 0:
                            P.op("act", lambda e: e.activation(out=av, in_=uv, func=AF.Copy), reads=[upB], writes=[accB])
                        else:
                            P.op("dve", lambda e: e.tensor_tensor(out=av, in0=av, in1=uv, op=ALU.add), reads=[upB, accB], writes=[accB])
                    LA_ = 2
                    for j in range(16 + LA_):
                        if j < 16:
                            stageA(j)
                        if j - LA_ >= 0:
                            stageB(j - LA_)
                    for s in range(NS):
                        k = c_i % 2; c_i += 1
                        csrc = I["ck%d" % g][la, s, 0:WIN[g]:d, :, ho, :]
                        P.dma("sp", lambda e, k=k, csrc=csrc: e.dma_start(out=cch[k][:, :].rearrange("p (a q) -> p a q", a=2), in_=csrc),
                              writes=[cchB[k]], owner=cchB[k])
                        P.op("pool", lambda e, k=k: e.tensor_copy(out=cchb[k][:, :], in_=cch[k][:, :]), reads=[cchB[k]], writes=[cchbB[k]])
                        tpp, tppB = nxt("tp")
                        tpb = tpp[:, :].bitcast(BF16)
                        P.op("pe", lambda e, k=k: e.transpose(tpb[:, 0:128], cchb[k][:, 0:128], idb[:]), reads=[cchbB[k], cB], writes=[tppB])
                        P.op("dve", lambda e: e.tensor_copy(out=kcT[:, :], in_=tpb[:, 0:128]), reads=[tppB], writes=[kcTB])
                        qs = QT[:, g, T + s:T + s + 1]
                        sp_, spB = nxt("sps")
                        P.op("pe", lambda e, sp_=sp_, qs=qs: e.matmul(sp_[:, 0:1], lhsT=kcT[:, :], rhs=qs, start=True, stop=True), reads=[kcTB, QTB], writes=[spB])
                        P.op("pe", lambda e, sp_=sp_, s=s: e.matmul(sp_[:, 8:9], lhsT=KVs[0:1, s, 0:128], rhs=onesb[0:1, 0:1], start=True, stop=True),
                             reads=[KVsB, cB], writes=[spB])
                        P.op("dve", lambda e, sp_=sp_: e.tensor_copy(out=knT[:, 0:1], in_=sp_[:, 8:9]), reads=[spB], writes=[knTB])
                        P.op("pe", lambda e, sp_=sp_, qs=qs: e.matmul(sp_[0:1, 16:17], lhsT=knT[:, 0:1], rhs=qs, start=True, stop=True), reads=[knTB, QTB], writes=[spB])
                        P.op("act", lambda e, sp_=sp_, g=g: e.activation(out=pts[:, 0:1], in_=sp_[:, 0:1], func=AF.Exp, bias=biasS[:, g, ho:ho + 1], scale=1.0),
                             reads=[spB, vecB], writes=[ptsB])
                        P.op("act", lambda e, sp_=sp_, g=g: e.activation(out=pts[0:1, 1:2], in_=sp_[0:1, 16:17], func=AF.Exp, bias=t5s[0:1, g * HO + ho:g * HO + ho + 1], scale=1.0),
                             reads=[spB, vecB], writes=[ptsB])
                        up, upB = nxt("ulp")
                        for which in range(2):
                            lhs = cchb[k][:, 128:256] if which == 0 else onesb[:, :]
                            lhs1 = KVs[0:1, s, 128:256] if which == 0 else onesb[0:1, :]
                            P.op("pe", lambda e, up=up, which=which, lhs=lhs: e.matmul(up[:, which:which + 1], lhsT=lhs, rhs=pts[:, 0:1], start=True, stop=False),
                                 reads=[cchbB[k], ptsB, cB], writes=[upB], sig=False)
                            P.op("pe", lambda e, up=up, which=which, lhs1=lhs1: e.matmul(up[:, which:which + 1], lhsT=lhs1, rhs=pts[0:1, 1:2], start=False, stop=True),
                                 reads=[KVsB, ptsB, cB], writes=[upB], sig=(which == 1))
                        av = acc[:, :, T + s]
                        if g == 0:
                            P.op("act", lambda e, av=av, up=up: e.activation(out=av, in_=up[:, 0:2], func=AF.Copy), reads=[upB], writes=[accB])
                        else:
                            P.op("dve", lambda e, av=av, up=up: e.tensor_tensor(out=av, in0=av, in1=up[:, 0:2], op=ALU.add), reads=[upB, accB], writes=[accB])
                P.op("dve", lambda e: e.reciprocal(out=rl[:, :], in_=acc[:, 1, :]), reads=[accB], writes=[rlB])
                P.op("dve", lambda e: e.tensor_tensor(out=rl[:, :], in0=rl[:, :], in1=acc[:, 0, :], op=ALU.mult), reads=[accB, rlB], writes=[rlB])
                P.op("dve", lambda e: e.tensor_tensor(out=gst[:, :], in0=rl[:, :], in1=siluz[:, :], op=ALU.mult), reads=[rlB, szB_], writes=[gstB])
                P.dma("sp", lambda e, ho=ho: e.dma_start(out=S["gT"][ho, :, :], in_=gst[:, :]), reads=[gstB], writes=[self.gTB[ho]], owner=gstB)

        self.outB = B("outputs")
        self.heads = list(range(HO))
        self.pool_groups = list(range(c.GO))
        import os
        stop = int(os.environ.get("KSTOP", "999"))
        phases = [("setup", setup)]
        for i in range(c.DEPTH):
            phases.append(("adaln%d" % i, lambda i=i: adaln(i)))
            phases.append(("norm%d" % i, lambda i=i: norm_phase(i)))
            if i % 2 == 0:
                phases.append(("pool%d" % i, lambda i=i: pool_phase(i)))
                phases.append(("wout%d" % i, lambda i=i: wout_phase("pool_w_out", i // 2, c.KCp, 2 if c.KCp > KC else 1)))
            else:
                phases.append(("att%d" % i, lambda i=i: att_phase(i)))
                phases.append(("wout%d" % i, lambda i=i: wout_phase("att_w_out", i // 2, HO, 1)))
        phases.append(("final", lambda: norm_phase(c.DEPTH)))
        for pi, (nm, fn) in enumerate(phases):
            if pi >= stop:
                break
            fn()
        if not dry and stop < 999:
            print("phases run:", [p[0] for p in phases[:stop]], "nops", P.nops)
        if not dry:
            P.barrier()
            P.emit()

    def att_cols(self, hg, blk):
        c = self.c
        QW = 3 * c.D

        def col(kind, g):
            if kind == "z":
                return 3 * QW + hg * 128
            off = {"q": 0, "k": QW, "v": 2 * QW}[kind]
            return off + g * c.D + hg * 128
        if blk == 0:
            st = [col("q", 0), col("q", 1)]
        elif blk == 1:
            st = [col("q", 2), col("z", 0)]
        else:
            st = [col("k", blk - 2), col("v", blk - 2)]
        return np.concatenate([np.arange(s, s + 128) for s in st])


def make_consts():
    oh = np.zeros((3, 33, FP), np.float32)
    ohs = np.zeros((3, 32, 128), np.float32)
    for g, d in enumerate(DIL):
        for j in range(FP):
            rel = j - 127
            if 0 <= rel <= 128:
                oh[g, int(t5_bucket(rel * d)), j] = 1.0
            else:
                oh[g, 32, j] = NEG
        for j in range(128):
            ohs[g, int(t5_bucket((128 - j) * d)), j] = 1.0
    invc = np.tile((1.0 / np.arange(1, 17, dtype=np.float32))[None, :], (128, 1)).astype(np.float32)
    fmask = np.ascontiguousarray(np.tile(oh[None, :, 32, :], (128, 1, 1)))
    oh[:, 32, :] = 0.0
    return oh, ohs, invc, fmask


def fm(v):
    return np.ascontiguousarray(np.asarray(v, np.float32).reshape(-1, 128).T)


_CACHE = {}


def get_program(cfg):
    key = (cfg.D, cfg.NS, cfg.TP, cfg.DEPTH)
    if key in _CACHE:
        return _CACHE[key]
    bld = Builder(cfg)
    nc0 = bass.Bass("TRN2", target_bir_lowering=False)
    bld.declare(nc0, 1)
    with contextlib.ExitStack() as es:
        bld.build(nc0, es, dry=True)
    specs = bld.specs
    nc = bass.Bass("TRN2", target_bir_lowering=False)
    bld.declare(nc, len(specs))
    with contextlib.ExitStack() as es:
        bld.build(nc, es, dry=False)
    _CACHE[key] = (nc, specs)
    return nc, specs


def build_wstream(specs, W):
    ws = np.zeros((len(specs), 128, WSLOT), np.float32)
    for n, sp in enumerate(specs):
        src = W[sp["src"]]
        l = sp["layer"]
        m = src[l] if not isinstance(l, tuple) else src[l[0]][l[1]]
        blk = m[sp["r0"]:sp["r0"] + sp["nrows"]][:, sp["cols"]]
        blk = blk.reshape(sp["kcb"], 128, sp["C"]).transpose(1, 0, 2).reshape(128, -1)
        ws[n, :, :blk.shape[1]] = blk
    return ws


def run(cfg, inp):
    c = cfg
    nc, specs = get_program(cfg)
    BATCH = inp["x_prompt"].shape[0]
    DEC = inp["x_sample"].shape[0]
    assert c.TP == 1 and DEC == BATCH * c.NS
    W = {k: np.asarray(inp[k], np.float32) for k in ("ada_w", "pool_w_in", "pool_w_grp", "pool_w_out", "att_w_in", "att_w_out")}
    wstream = build_wstream(specs, W)
    oh, ohs, invc, fmask = make_consts()
    f32 = lambda a: np.ascontiguousarray(np.asarray(a, np.float32))
    adab = f32(np.stack([fm(inp["ada_b"][i]) for i in range(c.DEPTH)]))
    npre = f32(np.stack([fm(inp["norm_pre"][i]) for i in range(c.DEPTH)]))
    npost = f32(np.stack([fm(inp["norm_post"][i]) for i in range(c.DEPTH)]))
    pscale = f32(np.stack([fm(inp["pool_scale"][l]) for l in range(c.LP)]))
    t5o = f32(np.asarray(inp["t5_bias"], np.float32))
    in_maps = []
    n_cores = 8
    for core in range(n_cores):
        b = core % BATCH
        ss = slice(b * c.NS, (b + 1) * c.NS)
        cst = np.concatenate([np.asarray(inp["c_prompt"][b:b + 1], np.float32), np.asarray(inp["c_sample"][ss], np.float32)], axis=0)
        cT = f32(cst.reshape(1 + c.NS, c.KC, 128).transpose(2, 1, 0))
        m = dict(xp=f32(inp["x_prompt"][b]), xs=f32(np.asarray(inp["x_sample"])[ss, 0, :]), cT=cT, adab=adab, npre=npre, npost=npost,
                 t5o=t5o, oh=oh, ohs=ohs, fmask=fmask, invc=invc, pscale=pscale, spool=f32(np.asarray(inp["state_pool"])[:, ss]),
                 ck0=f32(np.asarray(inp["cache_kv0"])[:, ss]), ck1=f32(np.asarray(inp["cache_kv1"])[:, ss]),
                 ck2=f32(np.asarray(inp["cache_kv2"])[:, ss]), wstream=wstream)
        in_maps.append(m)
    res = run_bass_kernel_spmd(nc, in_maps, core_ids=list(range(n_cores)))
    R = res.results
    yp = np.stack([R[b]["yp"] for b in range(BATCH)])
    ys = np.concatenate([R[b]["ys"] for b in range(BATCH)])[:, None, :]
    kvp = [np.stack([R[b]["kvp%d" % g] for b in range(BATCH)], axis=1) for g in range(3)]
    kvs = [np.concatenate([R[b]["kvs%d" % g] for b in range(BATCH)], axis=1)[:, :, None] for g in range(3)]
    poolp = np.stack([R[b]["poolp"] for b in range(BATCH)], axis=1)
    pools = np.concatenate([R[b]["pools"] for b in range(BATCH)], axis=1)
    outs = (yp, ys, kvp[0], kvp[1], kvp[2], poolp, kvs[0], kvs[1], kvs[2], pools)
    return tuple(np.ascontiguousarray(o.astype(np.float32)) for o in outs)


def kernel(**inputs):
    cfg = Cfg(D=2048, NS=2, TP=1, DEPTH=4)
    return run(cfg, inputs)
```

```python
import contextlib
import numpy as np
import concourse.bass as bass
import concourse.mybir as mybir
from concourse.bass_utils import run_bass_kernel_spmd

F32 = mybir.dt.float32
BF16 = mybir.dt.bfloat16
AF = mybir.ActivationFunctionType
ALU = mybir.AluOpType
AX = mybir.AxisListType

T = 2048
PB = 15
NEG = -30000.0
DIL = (1, 4, 16)
WIN = (128, 512, 2048)
PWINS = (2, 4, 8, 16)
N_BUCKETS = 32
T5_MAX_DIST = 2048
RMS_EPS = 1e-6
WSLOT = 4096
FP = 384
FREP = 130
ENGS = ("pe", "act", "dve", "pool", "sp")


def t5_bucket(dist):
    dist = np.asarray(dist, dtype=np.int64)
    max_exact = N_BUCKETS // 2
    ratio = np.log(np.maximum(dist, 1) / max_exact) / np.log(T5_MAX_DIST / max_exact)
    large = np.minimum(max_exact + (ratio * (N_BUCKETS - max_exact)).astype(np.int64), N_BUCKETS - 1)
    return np.where(dist < max_exact, dist, large).astype(np.int32)


class Cfg:
    def __init__(self, D=2048, NS=2, TP=1, DEPTH=4):
        self.D, self.NS, self.TP, self.DEPTH = D, NS, TP, DEPTH
        self.KC = D // 128
        self.H = D // 128
        self.HO = self.H // TP
        self.TT = T + NS
        self.PW = 2 * D
        self.PG = self.PW // 4
        self.GC = self.PG // 128
        self.GO = 4 // TP
        self.PWo = self.PW // TP
        self.KCp = self.PWo // 128
        self.LP = (DEPTH + 1) // 2
        self.LA = DEPTH // 2


class Buf:
    __slots__ = ("name", "w", "r", "sem", "dcnt")

    def __init__(self, name):
        self.name, self.w, self.r, self.sem, self.dcnt = name, None, [], None, 0


class _Rec:
    def __init__(self):
        self.spec = None

    def __getattr__(self, name):
        def f(*a, **k):
            self.spec = (name, a, k)
            return self
        return f


def _record(fn):
    r = _Rec()
    fn(r)
    assert r.spec is not None
    return r.spec


class Prog:
    def __init__(self, nc, es):
        self.nc, self.es = nc, es
        self.streams = {e: [] for e in ENGS}
        self.esem = {e: es.enter_context(nc.semaphore("S_" + e)) for e in ENGS}
        self.cnt = {e: 0 for e in ENGS}
        self.seen = {e: {} for e in ENGS}
        self.dry = False
        self.semtab = {}
        self.nops = 0
        self.klog = False
        import os
        self.maxops = int(os.environ.get('KMAXOPS', '100000000'))

    def _bsem(self, b):
        ent = self.semtab.get(b.name)
        if ent is None:
            ent = [self.es.enter_context(self.nc.semaphore("D_" + b.name)), 0]
            self.semtab[b.name] = ent
        return ent

    def _need(self, eng, sem, val):
        if self.klog and self.seen[eng].get(id(sem), 0) < val:
            print("   WAIT", eng, [k for k, v in self.semtab.items() if v[0] is sem] or [e for e in ENGS if self.esem[e] is sem], val)
        if self.seen[eng].get(id(sem), 0) < val:
            self.seen[eng][id(sem)] = val
            self.streams[eng].append(("wait", sem, val))

    def _waits(self, eng, reads, writes, skip):
        need = {}

        def add(ev):
            if ev[0] is skip:
                return
            k = id(ev[0])
            if k not in need or need[k][1] < ev[1]:
                need[k] = ev
        for b in reads:
            if b.w is not None:
                add(b.w)
        for b in writes:
            if b.w is not None:
                add(b.w)
            for ev in b.r:
                add(ev)
        for ev in need.values():
            self._need(eng, *ev)

    def op(self, eng, fn, reads=(), writes=(), sig=True):
        if self.dry:
            return
        self.nops += 1
        if self.nops > self.maxops:
            return
        fn = _record(fn)
        if self.klog:
            print("OP", self.nops, eng, fn[0], [getattr(a, "shape", None) for a in fn[1]], {k: (getattr(v, "shape", v)) for k, v in fn[2].items() if k in ("out", "in_", "lhsT", "rhs")})
        skip = self.esem[eng] if eng == "pe" else None
        self._waits(eng, reads, writes, skip)
        if sig:
            self.cnt[eng] += 1
            ev = (self.esem[eng], self.cnt[eng])
            self.streams[eng].append(("op", fn, self.esem[eng], 1))
        else:
            ev = (self.esem[eng], self.cnt[eng] + 1)
            self.streams[eng].append(("op", fn, None, 0))
        for b in writes:
            b.w, b.r = ev, []
        for b in reads:
            b.r.append(ev)

    def dma(self, q, fn, reads=(), writes=(), owner=None):
        if self.dry:
            return
        self.nops += 1
        if self.nops > self.maxops:
            return
        fn = _record(fn)
        if self.klog:
            print("OP", self.nops, q, "dma", {k: (getattr(v, "shape", v)) for k, v in fn[2].items() if k in ("out", "in_")})
        self._waits(q, reads, writes, None)
        ent = self._bsem(owner)
        ent[1] += 1
        sem = ent[0]
        ev = (sem, 16 * ent[1])
        self.streams[q].append(("op", fn, sem, 16))
        for b in writes:
            b.w, b.r = ev, []
        for b in reads:
            b.r.append(ev)

    def barrier(self):
        if self.dry:
            return
        for e in ENGS:
            for f in ("pe", "act", "dve", "pool"):
                if f != e and self.cnt[f] > 0:
                    self._need(e, self.esem[f], self.cnt[f])
            for ent in self.semtab.values():
                if ent[1] > 0:
                    self._need(e, ent[0], 16 * ent[1])

    def emit(self):
        nc, streams = self.nc, self.streams

        def replay(engobj, items):
            for it in items:
                if it[0] == "wait":
                    engobj.wait_ge(it[1], it[2])
                else:
                    ins = getattr(engobj, it[1][0])(*it[1][1], **it[1][2])
                    if it[2] is not None:
                        ins.then_inc(it[2], it[3])

        with nc.allow_non_contiguous_dma(reason="single sample-token columns / strided rows"), nc.Block() as block:
            @block.tensor
            def _(e):
                replay(e, streams["pe"])

            @block.scalar
            def _(e):
                replay(e, streams["act"])

            @block.vector
            def _(e):
                replay(e, streams["dve"])

            @block.gpsimd
            def _(e):
                replay(e, streams["pool"])

            @block.sync
            def _(e):
                replay(e, streams["sp"])


class Builder:
    def __init__(self, cfg):
        self.c = cfg
        self.specs = None

    def declare(self, nc, nblk):
        c = self.c
        di = lambda n, s, d=F32: nc.dram_tensor(n, list(s), d, kind="ExternalInput").ap()
        do = lambda n, s, d=F32: nc.dram_tensor(n, list(s), d, kind="ExternalOutput").ap()
        dn = lambda n, s, d=F32: nc.dram_tensor(n, list(s), d).ap()
        I = {}
        I["xp"] = di("xp", (T, c.D))
        I["xs"] = di("xs", (c.NS, c.D))
        I["cT"] = di("cT", (128, c.KC, 1 + c.NS))
        I["adab"] = di("adab", (c.DEPTH, 128, 3 * c.KC))
        I["npre"] = di("npre", (c.DEPTH, 128, c.KC))
        I["npost"] = di("npost", (c.DEPTH, 128, c.KC))
        I["t5o"] = di("t5o", (32, 3 * c.HO))
        I["oh"] = di("oh", (3, 33, FP))
        I["ohs"] = di("ohs", (3, 32, 128))
        I["fmask"] = di("fmask", (128, 3, FP))
        I["invc"] = di("invc", (128, 16))
        I["pscale"] = di("pscale", (c.LP, 128, c.KCp))
        I["spool"] = di("spool", (c.LP, c.NS, PB, c.PWo))
        for g in range(3):
            I["ck%d" % g] = di("ck%d" % g, (c.LA, c.NS, WIN[g], 2, c.HO, 128))
        I["wstream"] = di("wstream", (nblk, 128, WSLOT))
        O = {}
        O["yp"] = do("yp", (T, c.D))
        O["ys"] = do("ys", (c.NS, c.D))
        for g in range(3):
            O["kvp%d" % g] = do("kvp%d" % g, (c.LA, WIN[g], 2, c.HO, 128))
            O["kvs%d" % g] = do("kvs%d" % g, (c.LA, c.NS, 2, c.HO, 128))
        O["poolp"] = do("poolp", (c.LP, PB, c.PWo))
        O["pools"] = do("pools", (c.LP, c.NS, PB, c.PWo))
        S = {}
        S["xT"] = dn("xT", (c.KC, 128, c.TT))
        S["yT"] = dn("yT", (c.KC, 128, c.TT))
        S["gT"] = dn("gT", (max(c.KCp, c.HO), 128, c.TT), BF16)
        S["fd"] = dn("fd", (3 * c.HO, FREP * FP))
        self.I, self.O, self.S = I, O, S

    def build(self, nc, es, dry):
        c = self.c
        P = Prog(nc, es)
        P.dry = dry
        self.P, self.nc = P, nc
        if dry:
            self.specs = []
        self.wi = 0
        self.wseq = 0
        I, O, S = self.I, self.O, self.S
        KC, NS, TT, HO = c.KC, c.NS, c.TT, c.HO
        sbt = lambda n, s, d: es.enter_context(nc.sbuf_tensor("s_" + n, list(s), d))
        pst = lambda n, s, d: es.enter_context(nc.psum_tensor("p_" + n, list(s), d))
        B = lambda n: Buf(n)

        hraw = sbt("hraw", (128, KC * TT + 128), BF16); hTB = B("hT")
        hT = hraw[:, 0:KC * TT].rearrange("p (k t) -> p k t", k=KC)
        stage = [sbt("wst%d" % i, (128, WSLOT), F32) for i in range(2)]
        stB = [B("wst%d" % i) for i in range(2)]
        NWB = 3
        wbf = [sbt("wbf%d" % i, (128, WSLOT), BF16) for i in range(NWB)]
        wbB = [B("wbf%d" % i) for i in range(NWB)]
        idf = sbt("idf", (128, 128), F32); idb = sbt("idb", (128, 128), BF16)
        onesb = sbt("onesb", (128, 128), BF16); onesD = sbt("onesD", (128, 128), F32)
        onesf = sbt("onesf", (128, 128), F32)
        epsb = sbt("epsb", (128, 1), F32)
        cB = B("consts")
        cTs = sbt("cTs", (128, KC, 1 + NS), F32); scT = sbt("scT", (128, KC, 1 + NS), BF16); scB = B("scT")
        adab = sbt("adabs", (128, c.DEPTH, 3 * KC), F32)
        npre = sbt("npres", (128, c.DEPTH, KC), F32)
        npost = sbt("nposts", (128, c.DEPTH, KC), F32)
        vecB = B("vecs")
        modT = [sbt("modT%d" % i, (128, 3 * KC, 1 + NS), F32) for i in range(c.DEPTH)]
        modB = [B("modT%d" % i) for i in range(c.DEPTH)]
        tabA = [sbt("tabA%d" % i, (128, KC, 1 + NS), F32) for i in range(c.DEPTH)]
        tabG = [sbt("tabG%d" % i, (128, KC, 1 + NS), F32) for i in range(c.DEPTH)]
        tabB = [B("tab%d" % i) for i in range(c.DEPTH)]
        t5s = sbt("t5s", (32, 3 * HO), F32)
        biasS = sbt("biasS", (128, 3, HO), F32)
        invc = sbt("invc", (128, 16), F32)
        pscale = sbt("pscales", (128, c.LP, c.KCp), F32)
        ARENA = 19200
        arena = sbt("arena", (128, ARENA), F32)
        self.aoff = 0

        def areset():
            self.aoff = 0

        def aview(shape, dt):
            n = int(np.prod(shape))
            nel = n if dt == F32 else (n + 1) // 2
            off = self.aoff
            self.aoff += nel
            assert self.aoff <= ARENA, ("arena overflow", self.aoff)
            v = arena[:, off:off + nel]
            if dt == BF16:
                v = v.bitcast(BF16)[:, 0:n]
            if len(shape) == 2:
                v = v.rearrange("p (a b) -> p a b", a=shape[0])
            elif len(shape) == 3:
                v = v.rearrange("p (a b c) -> p a b c", a=shape[0], b=shape[1])
            return v

        banks = [pst("bk%d" % i, (128, 512), F32) for i in range(8)]
        bankB = [B("bk%d" % i) for i in range(8)]
        bankmap = {"pj": list(range(7)), "sps": [], "ulp": [], "tp": [7]}
        rot = {"pj": 0, "sps": 0, "ulp": 0, "tp": 0}

        def set_banks(att):
            if att:
                bankmap.update({"pj": [0, 1], "sps": [2, 3, 4], "ulp": [5, 6, 7], "tp": [0, 1]})
            else:
                bankmap.update({"pj": list(range(7)), "sps": [], "ulp": [], "tp": [7]})

        def nxt(kind):
            lst = bankmap[kind]
            key = "pj" if (kind == "tp" and bankmap["tp"] == bankmap["pj"]) else kind
            i = lst[rot[key] % len(lst)]
            rot[key] += 1
            return banks[i], bankB[i]

        ptiles = [(t0, 512) for t0 in range(0, T, 512)]
        alltiles = ptiles + [(T, NS)]

        def wload(n):
            sp = self.specs[n]
            ne = sp["kcb"] * sp["C"]
            s = n % 2
            P.dma("sp", lambda e, s=s, n=n, ne=ne: e.dma_start(out=stage[s][:, 0:ne], in_=I["wstream"][n, :, 0:ne]),
                  writes=[stB[s]], owner=stB[s])

        def wcast(n):
            sp = self.specs[n]
            ne = sp["kcb"] * sp["C"]
            s, d = n % 2, n % NWB
            P.op("pool", lambda e, s=s, d=d, ne=ne: e.tensor_copy(out=wbf[d][:, 0:ne], in_=stage[s][:, 0:ne]),
                 reads=[stB[s]], writes=[wbB[d]])

        def wget(src, layer, r0, nrows, cols):
            kcb, C = nrows // 128, len(cols)
            assert kcb * C <= WSLOT
            if dry:
                self.specs.append(dict(src=src, layer=layer, r0=r0, nrows=nrows, cols=np.asarray(cols), kcb=kcb, C=C))
                n = len(self.specs) - 1
            else:
                n = self.wi
                self.wi += 1
                nb_ = len(self.specs)
                seq_end = 2 * n + 4
                while self.wseq <= seq_end:
                    q = self.wseq
                    self.wseq += 1
                    if q < 2:
                        if q < nb_:
                            wload(q)
                    elif q % 2 == 0:
                        k = (q - 2) // 2
                        if k < nb_:
                            wcast(k)
                    else:
                        k = (q - 3) // 2 + 2
                        if k < nb_:
                            wload(k)
            d = n % NWB
            return wbf[d][:, 0:kcb * C].rearrange("p (k c) -> p k c", k=kcb), wbB[d]

        def setup():
            P.op("pool", lambda e: e.memset(idf[:], 0.0), writes=[cB])
            P.op("pool", lambda e: e.affine_select(out=idf[:], in_=idf[:], pattern=[[-1, 128]], compare_op=ALU.not_equal,
                                                   fill=1.0, base=0, channel_multiplier=1), reads=[cB], writes=[cB])
            P.op("pool", lambda e: e.tensor_copy(out=idb[:], in_=idf[:]), reads=[cB], writes=[cB])
            P.op("pool", lambda e: e.memset(onesb[:], 1.0), writes=[cB])
            P.op("pool", lambda e: e.memset(onesf[:], 1.0), writes=[cB])
            P.op("pool", lambda e: e.memset(onesD[:], 1.0 / c.D), writes=[cB])
            P.op("pool", lambda e: e.memset(epsb[:], RMS_EPS), writes=[cB])
            P.dma("sp", lambda e: e.dma_start(out=cTs[:], in_=I["cT"]), writes=[scB], owner=scB)
            P.op("act", lambda e: e.activation(out=scT[:], in_=cTs[:], func=AF.Silu), reads=[scB], writes=[scB])
            P.dma("sp", lambda e: e.dma_start(out=adab[:], in_=I["adab"].rearrange("l p k -> p l k")), writes=[vecB], owner=vecB)
            P.dma("sp", lambda e: e.dma_start(out=npre[:], in_=I["npre"].rearrange("l p k -> p l k")), writes=[vecB], owner=vecB)
            P.dma("sp", lambda e: e.dma_start(out=npost[:], in_=I["npost"].rearrange("l p k -> p l k")), writes=[vecB], owner=vecB)
            P.dma("sp", lambda e: e.dma_start(out=pscale[:], in_=I["pscale"].rearrange("l p k -> p l k")), writes=[vecB], owner=vecB)
            P.dma("sp", lambda e: e.dma_start(out=invc[:], in_=I["invc"]), writes=[vecB], owner=vecB)
            P.dma("sp", lambda e: e.dma_start(out=t5s[:], in_=I["t5o"]), writes=[vecB], owner=vecB)
            areset()
            ohsb = aview((3, FP), F32)
            ohss = aview((3, 128), F32)
            fsb = aview((3, FP), F32)[0:HO]
            fmk = aview((3, FP), F32)
            tB = B("t5tmp")
            P.dma("sp", lambda e: e.dma_start(out=ohsb[0:33], in_=I["oh"].rearrange("g b j -> b g j")), writes=[tB], owner=tB)
            P.dma("sp", lambda e: e.dma_start(out=ohss[0:32], in_=I["ohs"].rearrange("g b j -> b g j")), writes=[tB], owner=tB)
            P.dma("sp", lambda e: e.dma_start(out=fmk, in_=I["fmask"]), writes=[tB], owner=tB)
            fB = B("fsb")
            for g in range(3):
                ps, psB = nxt("pj")
                P.op("pe", lambda e, ps=ps, g=g: e.matmul(ps[0:HO, 0:FP], lhsT=t5s[:, g * HO:(g + 1) * HO], rhs=ohsb[0:32, g, :],
                                                           start=True, stop=True), reads=[vecB, tB], writes=[psB])
                P.op("dve", lambda e, ps=ps, g=g: e.tensor_tensor(out=fsb[:, g, :], in0=ps[0:HO, 0:FP], in1=fmk[0:HO, g, :], op=ALU.add),
                     reads=[psB, tB], writes=[fB])
                ps2, ps2B = nxt("pj")
                P.op("pe", lambda e, ps2=ps2, g=g: e.matmul(ps2[:, 0:HO], lhsT=ohss[0:32, g, :], rhs=t5s[:, g * HO:(g + 1) * HO],
                                                             start=True, stop=True), reads=[vecB, tB], writes=[ps2B])
                P.op("dve", lambda e, ps2=ps2, g=g: e.tensor_copy(out=biasS[:, g, :], in_=ps2[:, 0:HO]), reads=[ps2B], writes=[vecB])
            self.fdB = B("fd")
            import os
            for g in range(3):
                if os.environ.get("KSKIP_FD"):
                    break
                src = bass.AP(fsb.tensor, fsb[:, g, :].offset, [list(fsb.ap[0]), [0, FREP], [1, FP]])
                dst = S["fd"][g * HO:(g + 1) * HO, :].rearrange("h (r j) -> h r j", r=FREP)
                P.dma("sp", lambda e, src=src, dst=dst: e.dma_start(out=dst, in_=src), reads=[fB], writes=[self.fdB], owner=fB)

        def adaln(i):
            nb = 3 * c.D // 256
            for b in range(nb):
                wv, wB = wget("ada_w", i, 0, c.D, np.arange(b * 256, (b + 1) * 256))
                for m in range(2):
                    mc = b * 2 + m
                    ps, psB = nxt("pj")
                    for kc in range(KC):
                        P.op("pe", lambda e, ps=ps, wv=wv, kc=kc, m=m: e.matmul(
                            ps[:, 0:1 + NS], lhsT=wv[:, kc, m * 128:(m + 1) * 128], rhs=scT[:, kc, :],
                            start=(kc == 0), stop=(kc == KC - 1)), reads=[wB, scB], writes=[psB], sig=(kc == KC - 1))
                    P.op("dve", lambda e, ps=ps, mc=mc: e.tensor_scalar(
                        out=modT[i][:, mc, :], in0=ps[:, 0:1 + NS], scalar1=adab[:, i, mc:mc + 1], scalar2=0.0, op0=ALU.add, op1=ALU.add),
                        reads=[psB, vecB], writes=[modB[i]])
            a_ = npre[:, i, :]; b_ = npost[:, i, :]
            npb = bass.AP(a_.tensor, a_.offset, [list(a_.ap[0]), [1, KC], [0, 1 + NS]])
            npo = bass.AP(b_.tensor, b_.offset, [list(b_.ap[0]), [1, KC], [0, 1 + NS]])
            P.op("dve", lambda e: e.scalar_tensor_tensor(out=tabA[i][:], in0=modT[i][:, KC:2 * KC, :], scalar=1.0, in1=npb,
                                                         op0=ALU.add, op1=ALU.mult), reads=[modB[i], vecB], writes=[tabB[i]])
            P.op("dve", lambda e: e.tensor_tensor(out=tabG[i][:], in0=modT[i][:, 2 * KC:3 * KC, :], in1=npo, op=ALU.mult),
                 reads=[modB[i], vecB], writes=[tabB[i]])

        xTB = {}
        yTB = {}

        def dB(dct, key, nm):
            if key not in dct:
                dct[key] = B("%s_%s" % (nm, key))
            return dct[key]

        def rms_stats(src, n, sq, red, rstd, srcB, tmpB):
            sqB_, redB_, rsB_ = tmpB
            P.op("act", lambda e: e.activation(out=sq[:, :, 0:n], in_=src[:, :, 0:n], func=AF.Square), reads=[srcB], writes=[sqB_])
            P.op("dve", lambda e: e.tensor_reduce(out=red[:, 0:n], in_=sq[:, :, 0:n].rearrange("p k t -> p t k"),
                                                  axis=AX.X, op=ALU.add), reads=[sqB_], writes=[redB_])
            ps, psB = nxt("pj")
            P.op("pe", lambda e: e.matmul(ps[:, 0:n], lhsT=onesD[:], rhs=red[:, 0:n], start=True, stop=True),
                 reads=[redB_, cB], writes=[psB])
            P.op("act", lambda e: e.activation(out=rstd[:, 0:n], in_=ps[:, 0:n], func=AF.Sqrt, bias=epsb[:, 0:1], scale=1.0),
                 reads=[psB, cB], writes=[rsB_])
            P.op("dve", lambda e: e.reciprocal(out=rstd[:, 0:n], in_=rstd[:, 0:n]), reads=[rsB_], writes=[rsB_])

        def bc_k(v, n):
            return bass.AP(v.tensor, v.offset, [list(v.ap[0]), [0, KC], [1, n]])

        def norm_phase(i):
            P.barrier()
            areset()
            NT = 256
            xt = [aview((KC, NT), F32) for _ in range(2)]; xtB = [B("xt0"), B("xt1")]
            yt = aview((KC, NT), F32); ytB = B("yt")
            sq = aview((KC, NT), F32)
            red = aview((1, NT), F32)[:, 0, :]
            rstd = aview((1, NT), F32)[:, 0, :]
            tmpB = (B("nrm_sq"), B("nrm_red"), B("nrm_rs"))
            xin = aview((1, c.D), F32)[:, 0, :] if i in (0, c.DEPTH) else None
            xinB = B("xin")
            tiles = [(t0, NT, 0) for t0 in range(0, T, NT)] + [(T + s, 1, 1 + s) for s in range(NS)]
            for ti, (t0, n, r) in enumerate(tiles):
                x_, xB_ = xt[ti % 2], xtB[ti % 2]
                if i == 0:
                    nsub = (n + 127) // 128
                    for sub in range(nsub):
                        m = min(128, n - sub * 128)
                        srcrows = I["xp"][t0 + sub * 128:t0 + sub * 128 + m, :] if t0 < T else I["xs"][t0 - T:t0 - T + 1, :]
                        P.dma("sp", lambda e, srcrows=srcrows, m=m: e.dma_start(out=xin[0:m, :], in_=srcrows), writes=[xinB], owner=xinB)
                        for k4 in range(0, KC, 4):
                            nk = min(4, KC - k4)
                            tpp, tppB = nxt("tp")
                            for kk in range(nk):
                                P.op("pe", lambda e, kk=kk, k4=k4, m=m: e.transpose(tpp[:, kk * 128:kk * 128 + m], xin[0:m, (k4 + kk) * 128:(k4 + kk + 1) * 128], idf[0:m, 0:m]),
                                     reads=[xinB, cB], writes=[tppB], sig=(kk == nk - 1))
                            P.op("dve", lambda e, x_=x_, k4=k4, nk=nk, m=m, sub=sub: e.tensor_copy(
                                out=x_[:, k4:k4 + nk, sub * 128:sub * 128 + m],
                                in_=tpp[:, 0:nk * 128].rearrange("p (k t) -> p k t", k=nk)[:, :, 0:m]), reads=[tppB], writes=[xB_])
                else:
                    yb = dB(yTB, (t0 // 512) if t0 < T else 4 + (t0 - T), "yT"); xb = dB(xTB, ti, "xT")
                    P.dma("sp", lambda e, t0=t0, n=n: e.dma_start(out=yt[:, :, 0:n], in_=S["yT"][:, :, t0:t0 + n].rearrange("k p t -> p k t")),
                          reads=[yb], writes=[ytB], owner=ytB)
                    P.dma("sp", lambda e, t0=t0, n=n, x_=x_: e.dma_start(out=x_[:, :, 0:n], in_=S["xT"][:, :, t0:t0 + n].rearrange("k p t -> p k t")),
                          reads=[xb], writes=[xB_], owner=xB_)
                    rms_stats(yt, n, sq, red, rstd, ytB, tmpB)
                    P.op("dve", lambda e, n=n: e.tensor_tensor(out=yt[:, :, 0:n], in0=yt[:, :, 0:n], in1=bc_k(rstd[:, 0:n], n), op=ALU.mult),
                         reads=[ytB, tmpB[2]], writes=[ytB])
                    for kc in range(KC):
                        P.op("dve", lambda e, kc=kc, n=n, x_=x_, r=r: e.scalar_tensor_tensor(
                            out=x_[:, kc, 0:n], in0=yt[:, kc, 0:n], scalar=tabG[i - 1][:, kc, r:r + 1], in1=x_[:, kc, 0:n],
                            op0=ALU.mult, op1=ALU.add), reads=[ytB, xB_, tabB[i - 1]], writes=[xB_])
                if i < c.DEPTH:
                    xb = dB(xTB, ti, "xT")
                    P.dma("sp", lambda e, t0=t0, n=n, x_=x_: e.dma_start(out=S["xT"][:, :, t0:t0 + n].rearrange("k p t -> p k t"), in_=x_[:, :, 0:n]),
                          reads=[xB_], writes=[xb], owner=xB_)
                    rms_stats(x_, n, sq, red, rstd, xB_, tmpB)
                    P.op("dve", lambda e, n=n, x_=x_: e.tensor_tensor(out=sq[:, :, 0:n], in0=x_[:, :, 0:n], in1=bc_k(rstd[:, 0:n], n), op=ALU.mult),
                         reads=[xB_, tmpB[2], tmpB[0]], writes=[tmpB[0]])
                    for kc in range(KC):
                        P.op("act", lambda e, kc=kc, n=n, t0=t0, r=r: e.activation(
                            out=hT[:, kc, t0:t0 + n], in_=sq[:, kc, 0:n], func=AF.Identity,
                            scale=tabA[i][:, kc, r:r + 1], bias=modT[i][:, kc, r:r + 1]), reads=[tmpB[0], tabB[i], modB[i]], writes=[hTB])
                else:
                    nsub = (n + 127) // 128
                    for sub in range(nsub):
                        m = min(128, n - sub * 128)
                        for k4 in range(0, KC, 4):
                            nk = min(4, KC - k4)
                            tpp, tppB = nxt("tp")
                            for kk in range(nk):
                                P.op("pe", lambda e, kk=kk, k4=k4, m=m, sub=sub, x_=x_: e.transpose(
                                    tpp[0:m, kk * 128:(kk + 1) * 128], x_[:, k4 + kk, sub * 128:sub * 128 + m], idf[:]),
                                    reads=[xB_, cB], writes=[tppB], sig=(kk == nk - 1))
                            P.op("dve", lambda e, k4=k4, nk=nk, m=m: e.tensor_copy(out=xin[0:m, k4 * 128:(k4 + nk) * 128], in_=tpp[0:m, 0:nk * 128]),
                                 reads=[tppB], writes=[xinB])
                        dst = O["yp"][t0 + sub * 128:t0 + sub * 128 + m, :] if t0 < T else O["ys"][t0 - T:t0 - T + 1, :]
                        P.dma("sp", lambda e, dst=dst, m=m: e.dma_start(out=dst, in_=xin[0:m, :]), reads=[xinB], writes=[], owner=xinB)

        def feat_proj(wv, wB, coff, kcn, rhs_fn, rhsB, evac, tiles=alltiles):
            for (t0, n) in tiles:
                ps, psB = nxt("pj")
                for kc in range(kcn):
                    P.op("pe", lambda e, ps=ps, kc=kc, t0=t0, n=n: e.matmul(
                        ps[:, 0:n], lhsT=wv[:, kc, coff:coff + 128], rhs=rhs_fn(kc, t0, n), start=(kc == 0), stop=(kc == kcn - 1)),
                        reads=[wB, rhsB], writes=[psB], sig=(kc == kcn - 1))
                evac(ps, psB, t0, n)

        def tok_proj(wv, wB, c0, ncols, colsel, m, evac):
            ps, psB = nxt("pj")
            for kc in range(KC):
                P.op("pe", lambda e, ps=ps, kc=kc: e.matmul(ps[0:m, 0:ncols], lhsT=colsel(kc), rhs=wv[:, kc, c0:c0 + ncols],
                                                            start=(kc == 0), stop=(kc == KC - 1)),
                     reads=[wB, hTB], writes=[psB], sig=(kc == KC - 1))
            evac(ps, psB)

        def wout_phase(src, li, kcg, npass):
            P.barrier()
            set_banks(False)
            areset()
            yst = [aview((1, 512), F32)[:, 0, :] for _ in range(3)]; ystB = [B("yst%d" % k) for k in range(3)]
            Cw = WSLOT // (kcg * 128) * 128
            Cw = min(Cw, 256)
            nbw = c.D // Cw
            cnt = 0
            for ps_i in range(npass):
                if npass == 1:
                    tl = alltiles; c0, c1 = 0, TT
                elif ps_i == 0:
                    tl = ptiles[0:2]; c0, c1 = 0, 1024
                else:
                    tl = ptiles[2:4] + [(T, NS)]; c0, c1 = 1024, TT
                w = c1 - c0
                assert kcg * w <= KC * TT + 128
                gin = hraw[:, 0:kcg * w].rearrange("p (k t) -> p k t", k=kcg)
                for k in range(kcg):
                    P.dma("sp", lambda e, k=k, c0=c0, c1=c1, gin=gin: e.dma_start(out=gin[:, k, :], in_=S["gT"][k, :, c0:c1]),
                          reads=[self.gTB[k]], writes=[hTB], owner=hTB)
                for b in range(nbw):
                    wv, wB = wget(src, li, 0, kcg * 128, np.arange(b * Cw, (b + 1) * Cw))
                    for m in range(Cw // 128):
                        mc = b * (Cw // 128) + m

                        def ev(ps, psB, t0, n, mc=mc):
                            nonlocal cnt
                            k = cnt % 3; cnt += 1
                            if cnt % 2:
                                P.op("act", lambda e: e.activation(out=yst[k][:, 0:n], in_=ps[:, 0:n], func=AF.Copy), reads=[psB], writes=[ystB[k]])
                            else:
                                P.op("dve", lambda e: e.tensor_copy(out=yst[k][:, 0:n], in_=ps[:, 0:n]), reads=[psB], writes=[ystB[k]])
                            ti = (t0 // 512) if t0 < T else None
                            dsts = [dB(yTB, ti, "yT")] if ti is not None else [dB(yTB, 4 + s, "yT") for s in range(NS)]
                            P.dma("sp", lambda e, k=k, t0=t0, n=n, mc=mc: e.dma_start(out=S["yT"][mc, :, t0:t0 + n], in_=yst[k][:, 0:n]),
                                  reads=[ystB[k]], writes=dsts, owner=ystB[k])
                        feat_proj(wv, wB, m * 128, kcg, lambda kc, t0, n, gin=gin, c0=c0: gin[:, kc, t0 - c0:t0 - c0 + n], hTB, ev, tiles=tl)

        def pool_phase(i):
            li = i // 2
            P.barrier()
            areset()
            TP16 = 16 + T
            ubs = [aview((1, TP16), F32)[:, 0, :] for _ in range(2)]; ubBs = [B("ub0"), B("ub1")]
            sa = aview((1, TP16), F32)[:, 0, :]; saB = B("sa")
            sb_ = aview((1, TP16), F32)[:, 0, :]; sbB = B("sb")
            rT = aview((c.GC, TT), BF16); rTB = B("rT")
            sz = [aview((1, 512), F32)[:, 0, :] for _ in range(2)]; szB = [B("sz0"), B("sz1")]
            gst = [aview((1, TT), BF16)[:, 0, :]] * 2; gstB = [B("gst0")] * 2
            ust = [aview((1, 128), F32)[:, 0, :] for _ in range(2)]; ustB = [B("ust0"), B("ust1")]
            stt = aview((1, 128), F32)[:, 0, :]; sttB = B("stt")
            uext = aview((NS, 16), F32); uextB = B("uext")
            rs_ = aview((1, NS), F32)[:, 0, :]
            for q_ in range(2):
                P.op("pool", lambda e: e.memset(ubs[q_][:, 0:16], 0.0), writes=[ubBs[q_]])
            P.op("pool", lambda e: e.memset(sa[:, 0:16], 0.0), writes=[saB])
            P.op("pool", lambda e: e.memset(sb_[:, 0:16], 0.0), writes=[sbB])
            cpB = B("poolcp")
            for s in range(NS):
                P.dma("sp", lambda e, s=s: e.dma_start(out=O["pools"][li, s, 0:PB - 1, :], in_=I["spool"][li, s, 1:PB, :]),
                      writes=[], owner=cpB)
            self.gTB = [B("gT%d" % k) for k in range(c.KCp)]
            ucnt = 0
            self.zcnt = 0
            for go in range(c.GO):
                gw = self.pool_groups[go]
                w = PWINS[gw]
                for fc in range(c.GC):
                    fo = go * c.GC + fc
                    ub, ubB = ubs[fo % 2], ubBs[fo % 2]
                    if fc % 2 == 0:
                        nch = min(2, c.GC - fc)
                        cols = np.concatenate([np.arange(gw * c.PG + (fc + q) * 128, gw * c.PG + (fc + q + 1) * 128) for q in range(nch)])
                        wv, wB = wget("pool_w_in", li, 0, c.D, cols)
                    coff = (fc % 2) * 128

                    def ev_u(ps, psB, t0, n):
                        if t0 < T:
                            P.op("act", lambda e, ps=ps, t0=t0, n=n: e.activation(out=ub[:, 16 + t0:16 + t0 + n], in_=ps[:, 0:n], func=AF.Copy),
                                 reads=[psB], writes=[ubB])
                        else:
                            for s in range(NS):
                                P.op("act", lambda e, ps=ps, s=s: e.activation(out=uext[:, s, 15:16], in_=ps[:, s:s + 1], func=AF.Copy),
                                     reads=[psB], writes=[uextB])
                    feat_proj(wv, wB, coff, KC, lambda kc, t0, n: hT[:, kc, t0:t0 + n], hTB, ev_u)
                    k = ucnt % 2; ucnt += 1

                    def ev_rows(ps, psB, k=k, fo=fo):
                        P.op("dve", lambda e, ps=ps, k=k: e.tensor_copy(out=ust[k][0:PB + NS, :], in_=ps[0:PB + NS, 0:128]), reads=[psB], writes=[ustB[k]])
                        P.dma("sp", lambda e, k=k, fo=fo: e.dma_start(out=O["poolp"][li, :, fo * 128:(fo + 1) * 128], in_=ust[k][0:PB, :]),
                              reads=[ustB[k]], writes=[], owner=ustB[k])
                        for s in range(NS):
                            P.dma("sp", lambda e, k=k, fo=fo, s=s: e.dma_start(out=O["pools"][li, s, PB - 1:PB, fo * 128:(fo + 1) * 128],
                                                                               in_=ust[k][PB + s:PB + s + 1, :]),
                                  reads=[ustB[k]], writes=[], owner=ustB[k])
                    tok_proj(wv, wB, coff, 128, lambda kc: hT[:, kc, T - PB:T + NS], PB + NS, ev_rows)
                    for s in range(NS):
                        P.dma("sp", lambda e, s=s, fo=fo: e.dma_start(out=stt[0:PB, :], in_=I["spool"][li, s, :, fo * 128:(fo + 1) * 128]),
                              writes=[sttB], owner=sttB)
                        tpp, tppB = nxt("tp")
                        P.op("pe", lambda e: e.transpose(tpp[:, 0:PB], stt[0:PB, :], idf[0:PB, 0:PB]), reads=[sttB, cB], writes=[tppB])
                        P.op("dve", lambda e, s=s: e.tensor_copy(out=uext[:, s, 0:PB], in_=tpp[:, 0:PB]), reads=[tppB], writes=[uextB])
                    cur, curB = ub, ubB
                    bufs = [(sa, saB), (sb_, sbB)]
                    sh = 1
                    bi = 0
                    while sh < w:
                        dst, dstB = bufs[bi % 2]; bi += 1
                        P.op("dve" if bi % 2 else "pool", lambda e, dst=dst, cur=cur, sh=sh: e.tensor_tensor(
                            out=dst[:, 16:TP16], in0=cur[:, 16:TP16], in1=cur[:, 16 - sh:TP16 - sh], op=ALU.add),
                            reads=[curB], writes=[dstB])
                        cur, curB = dst, dstB
                        sh *= 2
                    fin, finB = bufs[bi % 2]
                    P.op("dve", lambda e, fin=fin, cur=cur, w=w: e.scalar_tensor_tensor(
                        out=fin[:, 16:TP16], in0=cur[:, 16:TP16], scalar=1.0 / w, in1=ub[:, 16:TP16], op0=ALU.mult, op1=ALU.subtract),
                        reads=[curB, ubB], writes=[finB])
                    P.op("dve", lambda e, fin=fin, cur=cur, w=w: e.tensor_tensor(out=fin[:, 16:16 + w - 1], in0=cur[:, 16:16 + w - 1], in1=invc[:, 0:w - 1], op=ALU.mult),
                         reads=[curB, vecB], writes=[finB])
                    P.op("dve", lambda e, fin=fin, w=w: e.tensor_tensor(out=fin[:, 16:16 + w - 1], in0=fin[:, 16:16 + w - 1], in1=ub[:, 16:16 + w - 1], op=ALU.subtract),
                         reads=[ubB], writes=[finB])
                    P.op("act", lambda e, fin=fin, fc=fc: e.activation(out=rT[:, fc, 0:T], in_=fin[:, 16:TP16], func=AF.Copy), reads=[finB], writes=[rTB])
                    P.op("dve", lambda e, w=w: e.tensor_reduce(out=rs_[:, 0:NS], in_=uext[:, :, 16 - w:16], axis=AX.X, op=ALU.add), reads=[uextB], writes=[uextB])
                    P.op("dve", lambda e, w=w, fc=fc: e.scalar_tensor_tensor(out=rT[:, fc, T:TT], in0=rs_[:, 0:NS], scalar=1.0 / w, in1=uext[:, :, 15],
                                                                             op0=ALU.mult, op1=ALU.subtract), reads=[uextB], writes=[rTB])
                Cg = min(256, WSLOT // c.GC // 128 * 128, c.PG)
                for bg in range(c.PG // Cg):
                    wg, wgB = wget("pool_w_grp", (li, gw), 0, c.PG, np.arange(bg * Cg, (bg + 1) * Cg))
                    for zb in range(0, Cg // 128, 2):
                        nz = min(2, Cg // 128 - zb)
                        zc0 = c.PW + gw * c.PG + bg * Cg + zb * 128
                        wz, wzB = wget("pool_w_in", li, 0, c.D, np.arange(zc0, zc0 + nz * 128))
                        for q in range(nz):
                            m = zb + q
                            fo = go * c.GC + bg * (Cg // 128) + m
                            k = fo % 2
                            for (t0, n) in alltiles:
                                pa, paB = nxt("pj")
                                for kc in range(c.GC):
                                    P.op("pe", lambda e, pa=pa, kc=kc, m=m, t0=t0, n=n: e.matmul(
                                        pa[:, 0:n], lhsT=wg[:, kc, m * 128:(m + 1) * 128], rhs=rT[:, kc, t0:t0 + n],
                                        start=(kc == 0), stop=(kc == c.GC - 1)), reads=[wgB, rTB], writes=[paB], sig=(kc == c.GC - 1))
                                pz, pzB = nxt("pj")
                                for kc in range(KC):
                                    P.op("pe", lambda e, pz=pz, kc=kc, q=q, t0=t0, n=n: e.matmul(
                                        pz[:, 0:n], lhsT=wz[:, kc, q * 128:(q + 1) * 128], rhs=hT[:, kc, t0:t0 + n],
                                        start=(kc == 0), stop=(kc == KC - 1)), reads=[wzB, hTB], writes=[pzB], sig=(kc == KC - 1))
                                zi = self.zcnt % 2; self.zcnt += 1
                                P.op("act", lambda e, pz=pz, zi=zi, n=n: e.activation(out=sz[zi][:, 0:n], in_=pz[:, 0:n], func=AF.Silu),
                                     reads=[pzB], writes=[szB[zi]])
                                P.op("dve", lambda e, pa=pa, zi=zi, n=n, t0=t0, k=k, fo=fo: e.scalar_tensor_tensor(
                                    out=gst[k][:, t0:t0 + n], in0=pa[:, 0:n], scalar=pscale[:, li, fo:fo + 1], in1=sz[zi][:, 0:n],
                                    op0=ALU.mult, op1=ALU.mult), reads=[paB, szB[zi], vecB], writes=[gstB[k]])
                            P.dma("sp", lambda e, k=k, fo=fo: e.dma_start(out=S["gT"][fo, :, :], in_=gst[k][:, :]),
                                  reads=[gstB[k]], writes=[self.gTB[fo]], owner=gstB[k])

        def att_phase(i):
            la = i // 2
            P.barrier()
            set_banks(True)
            areset()
            QT = aview((3, TT), BF16); QTB = B("QT")
            siluz = aview((1, TT), BF16)[:, 0, :]; szB_ = B("siluz")
            acc = aview((2, TT), F32); accB = B("acc")
            KV = aview((16, 256), BF16); KVB = B("KV")
            KTt = aview((16, 128), BF16); KTB = B("KT")
            KVs = aview((NS, 256), BF16); KVsB = B("KVs")
            gst = aview((1, TT), BF16)[:, 0, :]; gstB = B("gst")
            biasT = aview((3, 256), F32); biasB = B("biasT")
            Ssb = [aview((1, 256), F32)[:, 0, :] for _ in range(4)]; SsbB = [B("Ssb%d" % k) for k in range(4)]
            PT = [aview((1, 256), BF16)[:, 0, :] for _ in range(4)]; PTB = [B("PT%d" % k) for k in range(4)]
            self.blk_i = 0
            kvst = [aview((1, 256), F32)[:, 0, :] for _ in range(3)]; kvstB = [B("kvst%d" % k) for k in range(3)]
            cch = [aview((1, 256), F32)[:, 0, :] for _ in range(2)]; cchB = [B("cch0"), B("cch1")]
            cchb = [aview((1, 256), BF16)[:, 0, :] for _ in range(2)]; cchbB = [B("cchb0"), B("cchb1")]
            kcT = aview((1, 128), BF16)[:, 0, :]; kcTB = B("kcT")
            knT = aview((1, 4), BF16)[:, 0, :]; knTB = B("knT")
            pts = aview((1, 4), BF16)[:, 0, :]; ptsB = B("pts")
            rl = aview((1, TT), F32)[:, 0, :]; rlB = B("rl")
            self.gTB = [B("gT%d" % k) for k in range(HO)]
            blk_i = 0
            kv_i = 0
            c_i = 0
            for ho in range(HO):
                hg = self.heads[ho]
                for g in range(3):
                    row = g * HO + ho
                    src = bass.AP(S["fd"].tensor, S["fd"][row, :].offset + 127, [[FP - 1, 128], [128, 2], [1, 128]])
                    P.dma("sp", lambda e, g=g, src=src: e.dma_start(out=biasT[:, g, :].rearrange("p (a q) -> p a q", a=2), in_=src),
                          reads=[self.fdB], writes=[biasB], owner=biasB)
                for (blk, items) in ((0, (("q", 0), ("q", 1))), (1, (("q", 2), ("z", 0)))):
                    wv, wB = wget("att_w_in", la, 0, c.D, self.att_cols(hg, blk))
                    for q, (kind, g) in enumerate(items):
                        if kind == "q":
                            def ev(ps, psB, t0, n, g=g):
                                P.op("act", lambda e, ps=ps, t0=t0, n=n, g=g: e.activation(out=QT[:, g, t0:t0 + n], in_=ps[:, 0:n], func=AF.Copy, scale=128 ** -0.5),
                                     reads=[psB], writes=[QTB])
                        else:
                            def ev(ps, psB, t0, n):
                                P.op("act", lambda e, ps=ps, t0=t0, n=n: e.activation(out=siluz[:, t0:t0 + n], in_=ps[:, 0:n], func=AF.Silu),
                                     reads=[psB], writes=[szB_])
                        feat_proj(wv, wB, q * 128, KC, lambda kc, t0, n: hT[:, kc, t0:t0 + n], hTB, ev)
                for g in range(3):
                    d = DIL[g]
                    nb = 16 // d
                    wv, wB = wget("att_w_in", la, 0, c.D, self.att_cols(hg, 2 + g))
                    for j in range(16):
                        r, cb = j // nb, j % nb
                        s0 = cb * 128 * d + r

                        def ev_kv(ps, psB, j=j, r=r, cb=cb, g=g):
                            nonlocal kv_i
                            P.op("act", lambda e, ps=ps, j=j: e.activation(out=KV[:, j, :], in_=ps[:, 0:256], func=AF.Copy), reads=[psB], writes=[KVB])
                            keep = (g == 2) or (g == 1 and cb == nb - 1) or (g == 0 and cb == 15)
                            if keep:
                                k = kv_i % 3; kv_i += 1
                                P.op("act", lambda e, ps=ps, k=k: e.activation(out=kvst[k][:, :], in_=ps[:, 0:256], func=AF.Copy), reads=[psB], writes=[kvstB[k]])
                                dst = O["kvp%d" % g][la, r:WIN[g]:d, :, ho, :] if g > 0 else O["kvp0"][la, :, :, ho, :]
                                P.dma("sp", lambda e, k=k, dst=dst: e.dma_start(out=dst, in_=kvst[k][:, :].rearrange("p (a q) -> p a q", a=2)),
                                      reads=[kvstB[k]], writes=[], owner=kvstB[k])
                        tok_proj(wv, wB, 0, 256, lambda kc, s0=s0, d=d: hT[:, kc, s0:s0 + 127 * d + 1:d], 128, ev_kv)
                    for s in range(NS):
                        def ev_kvs(ps, psB, s=s, g=g):
                            nonlocal kv_i
                            P.op("act", lambda e, ps=ps, s=s: e.activation(out=KVs[0:1, s, :], in_=ps[0:1, 0:256], func=AF.Copy), reads=[psB], writes=[KVsB])
                            k = kv_i % 3; kv_i += 1
                            P.op("act", lambda e, ps=ps, k=k: e.activation(out=kvst[k][0:1, :], in_=ps[0:1, 0:256], func=AF.Copy), reads=[psB], writes=[kvstB[k]])
                            P.dma("sp", lambda e, k=k, s=s, g=g: e.dma_start(out=O["kvs%d" % g][la, s:s + 1, :, ho, :],
                                                                             in_=kvst[k][0:1, :].rearrange("p (a q) -> p a q", a=2)),
                                  reads=[kvstB[k]], writes=[], owner=kvstB[k])
                        tok_proj(wv, wB, 0, 256, lambda kc, s=s: hT[:, kc, T + s:T + s + 1], 1, ev_kvs)
                    for j4 in range(0, 16, 4):
                        tpp, tppB = nxt("tp")
                        tpb = tpp[:, :].bitcast(BF16)
                        for jj in range(4):
                            P.op("pe", lambda e, j4=j4, jj=jj: e.transpose(tpb[:, jj * 128:(jj + 1) * 128], KV[:, j4 + jj, 0:128], idb[:]),
                                 reads=[KVB, cB], writes=[tppB], sig=(jj == 3))
                        P.op("dve", lambda e, j4=j4: e.tensor_copy(out=KTt[:, j4:j4 + 4, :], in_=tpb[:, 0:512].rearrange("p (a q) -> p a q", a=4)),
                             reads=[tppB], writes=[KTB])
                    st = {}

                    def stageA(j, g=g, d=d, nb=nb):
                        r, cb = j // nb, j % nb
                        s0 = cb * 128 * d + r
                        chunks = ([j - 1] if cb > 0 else []) + [j]
                        nch = len(chunks)
                        k = self.blk_i % 4; self.blk_i += 1
                        sp_, spB = nxt("sps")
                        up_, upB_ = nxt("ulp")
                        qv = QT[:, g, s0:s0 + 127 * d + 1:d]
                        for ci, jk in enumerate(chunks):
                            pos = (1 if (nch == 2 and ci == 0) else 0)
                            P.op("pe", lambda e: e.matmul(sp_[:, pos * 128:(pos + 1) * 128], lhsT=KTt[:, jk, :], rhs=qv, start=True, stop=True),
                                 reads=[KTB, QTB], writes=[spB], sig=(ci == nch - 1))
                        P.op("dve", lambda e: e.tensor_tensor(out=Ssb[k][:, 0:nch * 128], in0=sp_[:, 0:nch * 128], in1=biasT[:, g, 0:nch * 128], op=ALU.add),
                             reads=[spB, biasB], writes=[SsbB[k]])
                        P.op("act", lambda e: e.activation(out=PT[k][:, 0:nch * 128], in_=Ssb[k][:, 0:nch * 128], func=AF.Exp),
                             reads=[SsbB[k]], writes=[PTB[k]])
                        st[j] = (k, chunks, s0, up_, upB_)

                    def stageB(j, g=g, d=d):
                        k, chunks, s0, up, upB = st[j]
                        nch = len(chunks)
                        for which in range(2):
                            for ci, jk in enumerate(chunks):
                                pos = (1 if (nch == 2 and ci == 0) else 0)
                                lhs = KV[:, jk, 128:256] if which == 0 else onesb[:, :]
                                P.op("pe", lambda e: e.matmul(
                                    up[:, which * 128:(which + 1) * 128], lhsT=lhs, rhs=PT[k][:, pos * 128:(pos + 1) * 128],
                                    start=(ci == 0), stop=(ci == nch - 1)), reads=[KVB, PTB[k], cB], writes=[upB],
                                    sig=(which == 1 and ci == nch - 1))
                        av = acc[:, :, s0:s0 + 127 * d + 1:d]
                        uv = up[:, 0:256].rearrange("p (a q) -> p a q", a=2)
                        if g == 0:
                            P.op("act", lambda e: e.activation(out=av, in_=uv, func=AF.Copy), reads=[upB], writes=[accB])
                        else:
                            P.op("dve", lambda e: e.tensor_tensor(out=av, in0=av, in1=uv, op=ALU.add), reads=[upB, accB], writes=[accB])
                    LA_ = 2
                    for j in range(16 + LA_):
                        if j < 16:
                            stageA(j)
                        if j - LA_ >= 0:
                            stageB(j - LA_)
                    for s in range(NS):
                        k = c_i % 2; c_i += 1
                        csrc = I["ck%d" % g][la, s, 0:WIN[g]:d, :, ho, :]
                        P.dma("sp", lambda e, k=k, csrc=csrc: e.dma_start(out=cch[k][:, :].rearrange("p (a q) -> p a q", a=2), in_=csrc),
                              writes=[cchB[k]], owner=cchB[k])
                        P.op("pool", lambda e, k=k: e.tensor_copy(out=cchb[k][:, :], in_=cch[k][:, :]), reads=[cchB[k]], writes=[cchbB[k]])
                        tpp, tppB = nxt("tp")
                        tpb = tpp[:, :].bitcast(BF16)
                        P.op("pe", lambda e, k=k: e.transpose(tpb[:, 0:128], cchb[k][:, 0:128], idb[:]), reads=[cchbB[k], cB], writes=[tppB])
                        P.op("dve", lambda e: e.tensor_copy(out=kcT[:, :], in_=tpb[:, 0:128]), reads=[tppB], writes=[kcTB])
                        qs = QT[:, g, T + s:T + s + 1]
                        sp_, spB = nxt("sps")
                        P.op("pe", lambda e, sp_=sp_, qs=qs: e.matmul(sp_[:, 0:1], lhsT=kcT[:, :], rhs=qs, start=True, stop=True), reads=[kcTB, QTB], writes=[spB])
                        P.op("pe", lambda e, sp_=sp_, s=s: e.matmul(sp_[:, 8:9], lhsT=KVs[0:1, s, 0:128], rhs=onesb[0:1, 0:1], start=True, stop=True),
                             reads=[KVsB, cB], writes=[spB])
                        P.op("dve", lambda e, sp_=sp_: e.tensor_copy(out=knT[:, 0:1], in_=sp_[:, 8:9]), reads=[spB], writes=[knTB])
                        P.op("pe", lambda e, sp_=sp_, qs=qs: e.matmul(sp_[0:1, 16:17], lhsT=knT[:, 0:1], rhs=qs, start=True, stop=True), reads=[knTB, QTB], writes=[spB])
                        P.op("act", lambda e, sp_=sp_, g=g: e.activation(out=pts[:, 0:1], in_=sp_[:, 0:1], func=AF.Exp, bias=biasS[:, g, ho:ho + 1], scale=1.0),
                             reads=[spB, vecB], writes=[ptsB])
                        P.op("act", lambda e, sp_=sp_, g=g: e.activation(out=pts[0:1, 1:2], in_=sp_[0:1, 16:17], func=AF.Exp, bias=t5s[0:1, g * HO + ho:g * HO + ho + 1], scale=1.0),
                             reads=[spB, vecB], writes=[ptsB])
                        up, upB = nxt("ulp")
                        for which in range(2):
                            lhs = cchb[k][:, 128:256] if which == 0 else onesb[:, :]
                            lhs1 = KVs[0:1, s, 128:256] if which == 0 else onesb[0:1, :]
                            P.op("pe", lambda e, up=up, which=which, lhs=lhs: e.matmul(up[:, which:which + 1], lhsT=lhs, rhs=pts[:, 0:1], start=True, stop=False),
                                 reads=[cchbB[k], ptsB, cB], writes=[upB], sig=False)
                            P.op("pe", lambda e, up=up, which=which, lhs1=lhs1: e.matmul(up[:, which:which + 1], lhsT=lhs1, rhs=pts[0:1, 1:2], start=False, stop=True),
                                 reads=[KVsB, ptsB, cB], writes=[upB], sig=(which == 1))
                        av = acc[:, :, T + s]
                        if g == 0:
                            P.op("act", lambda e, av=av, up=up: e.activation(out=av, in_=up[:, 0:2], func=AF.Copy), reads=[upB], writes=[accB])
                        else:
                            P.op("dve", lambda e, av=av, up=up: e.tensor_tensor(out=av, in0=av, in1=up[:, 0:2], op=ALU.add), reads=[upB, accB], writes=[accB])
                P.op("dve", lambda e: e.reciprocal(out=rl[:, :], in_=acc[:, 1, :]), reads=[accB], writes=[rlB])
                P.op("dve", lambda e: e.tensor_tensor(out=rl[:, :], in0=rl[:, :], in1=acc[:, 0, :], op=ALU.mult), reads=[accB, rlB], writes=[rlB])
                P.op("dve", lambda e: e.tensor_tensor(out=gst[:, :], in0=rl[:, :], in1=siluz[:, :], op=ALU.mult), reads=[rlB, szB_], writes=[gstB])
                P.dma("sp", lambda e, ho=ho: e.dma_start(out=S["gT"][ho, :, :], in_=gst[:, :]), reads=[gstB], writes=[self.gTB[ho]], owner=gstB)

        self.outB = B("outputs")
        self.heads = list(range(HO))
        self.pool_groups = list(range(c.GO))
        import os
        stop = int(os.environ.get("KSTOP", "999"))
        phases = [("setup", setup)]
        for i in range(c.DEPTH):
            phases.append(("adaln%d" % i, lambda i=i: adaln(i)))
            phases.append(("norm%d" % i, lambda i=i: norm_phase(i)))
            if i % 2 == 0:
                phases.append(("pool%d" % i, lambda i=i: pool_phase(i)))
                phases.append(("wout%d" % i, lambda i=i: wout_phase("pool_w_out", i // 2, c.KCp, 2 if c.KCp > KC else 1)))
            else:
                phases.append(("att%d" % i, lambda i=i: att_phase(i)))
                phases.append(("wout%d" % i, lambda i=i: wout_phase("att_w_out", i // 2, HO, 1)))
        phases.append(("final", lambda: norm_phase(c.DEPTH)))
        for pi, (nm, fn) in enumerate(phases):
            if pi >= stop:
                break
            fn()
        if not dry and stop < 999:
            print("phases run:", [p[0] for p in phases[:stop]], "nops", P.nops)
        if not dry:
            P.barrier()
            P.emit()

    def att_cols(self, hg, blk):
        c = self.c
        QW = 3 * c.D

        def col(kind, g):
            if kind == "z":
                return 3 * QW + hg * 128
            off = {"q": 0, "k": QW, "v": 2 * QW}[kind]
            return off + g * c.D + hg * 128
        if blk == 0:
            st = [col("q", 0), col("q", 1)]
        elif blk == 1:
            st = [col("q", 2), col("z", 0)]
        else:
            st = [col("k", blk - 2), col("v", blk - 2)]
        return np.concatenate([np.arange(s, s + 128) for s in st])


def make_consts():
    oh = np.zeros((3, 33, FP), np.float32)
    ohs = np.zeros((3, 32, 128), np.float32)
    for g, d in enumerate(DIL):
        for j in range(FP):
            rel = j - 127
            if 0 <= rel <= 128:
                oh[g, int(t5_bucket(rel * d)), j] = 1.0
            else:
                oh[g, 32, j] = NEG
        for j in range(128):
            ohs[g, int(t5_bucket((128 - j) * d)), j] = 1.0
    invc = np.tile((1.0 / np.arange(1, 17, dtype=np.float32))[None, :], (128, 1)).astype(np.float32)
    fmask = np.ascontiguousarray(np.tile(oh[None, :, 32, :], (128, 1, 1)))
    oh[:, 32, :] = 0.0
    return oh, ohs, invc, fmask


def fm(v):
    return np.ascontiguousarray(np.asarray(v, np.float32).reshape(-1, 128).T)


_CACHE = {}


def get_program(cfg):
    key = (cfg.D, cfg.NS, cfg.TP, cfg.DEPTH)
    if key in _CACHE:
        return _CACHE[key]
    bld = Builder(cfg)
    nc0 = bass.Bass("TRN2", target_bir_lowering=False)
    bld.declare(nc0, 1)
    with contextlib.ExitStack() as es:
        bld.build(nc0, es, dry=True)
    specs = bld.specs
    nc = bass.Bass("TRN2", target_bir_lowering=False)
    bld.declare(nc, len(specs))
    with contextlib.ExitStack() as es:
        bld.build(nc, es, dry=False)
    _CACHE[key] = (nc, specs)
    return nc, specs


def build_wstream(specs, W):
    ws = np.zeros((len(specs), 128, WSLOT), np.float32)
    for n, sp in enumerate(specs):
        src = W[sp["src"]]
        l = sp["layer"]
        m = src[l] if not isinstance(l, tuple) else src[l[0]][l[1]]
        blk = m[sp["r0"]:sp["r0"] + sp["nrows"]][:, sp["cols"]]
        blk = blk.reshape(sp["kcb"], 128, sp["C"]).transpose(1, 0, 2).reshape(128, -1)
        ws[n, :, :blk.shape[1]] = blk
    return ws


def run(cfg, inp):
    c = cfg
    nc, specs = get_program(cfg)
    BATCH = inp["x_prompt"].shape[0]
    DEC = inp["x_sample"].shape[0]
    assert c.TP == 1 and DEC == BATCH * c.NS
    W = {k: np.asarray(inp[k], np.float32) for k in ("ada_w", "pool_w_in", "pool_w_grp", "pool_w_out", "att_w_in", "att_w_out")}
    wstream = build_wstream(specs, W)
    oh, ohs, invc, fmask = make_consts()
    f32 = lambda a: np.ascontiguousarray(np.asarray(a, np.float32))
    adab = f32(np.stack([fm(inp["ada_b"][i]) for i in range(c.DEPTH)]))
    npre = f32(np.stack([fm(inp["norm_pre"][i]) for i in range(c.DEPTH)]))
    npost = f32(np.stack([fm(inp["norm_post"][i]) for i in range(c.DEPTH)]))
    pscale = f32(np.stack([fm(inp["pool_scale"][l]) for l in range(c.LP)]))
    t5o = f32(np.asarray(inp["t5_bias"], np.float32))
    in_maps = []
    n_cores = 8
    for core in range(n_cores):
        b = core % BATCH
        ss = slice(b * c.NS, (b + 1) * c.NS)
        cst = np.concatenate([np.asarray(inp["c_prompt"][b:b + 1], np.float32), np.asarray(inp["c_sample"][ss], np.float32)], axis=0)
        cT = f32(cst.reshape(1 + c.NS, c.KC, 128).transpose(2, 1, 0))
        m = dict(xp=f32(inp["x_prompt"][b]), xs=f32(np.asarray(inp["x_sample"])[ss, 0, :]), cT=cT, adab=adab, npre=npre, npost=npost,
                 t5o=t5o, oh=oh, ohs=ohs, fmask=fmask, invc=invc, pscale=pscale, spool=f32(np.asarray(inp["state_pool"])[:, ss]),
                 ck0=f32(np.asarray(inp["cache_kv0"])[:, ss]), ck1=f32(np.asarray(inp["cache_kv1"])[:, ss]),
                 ck2=f32(np.asarray(inp["cache_kv2"])[:, ss]), wstream=wstream)
        in_maps.append(m)
    res = run_bass_kernel_spmd(nc, in_maps, core_ids=list(range(n_cores)))
    R = res.results
    yp = np.stack([R[b]["yp"] for b in range(BATCH)])
    ys = np.concatenate([R[b]["ys"] for b in range(BATCH)])[:, None, :]
    kvp = [np.stack([R[b]["kvp%d" % g] for b in range(BATCH)], axis=1) for g in range(3)]
    kvs = [np.concatenate([R[b]["kvs%d" % g] for b in range(BATCH)], axis=1)[:, :, None] for g in range(3)]
    poolp = np.stack([R[b]["poolp"] for b in range(BATCH)], axis=1)
    pools = np.concatenate([R[b]["pools"] for b in range(BATCH)], axis=1)
    outs = (yp, ys, kvp[0], kvp[1], kvp[2], poolp, kvs[0], kvs[1], kvs[2], pools)
    return tuple(np.ascontiguousarray(o.astype(np.float32)) for o in outs)


def kernel(**inputs):
    cfg = Cfg(D=2048, NS=2, TP=1, DEPTH=4)
    return run(cfg, inputs)
```
